# Optimizing a Trainium2 kernel written in Bass

```python
import jax, jax.numpy as jnp
from jax import lax
import numpy as np

D_MODEL = 1024
BATCH = 4
SEQ = 8192
DEPTH = 4

HEAD_DIM = 64
D_RNN = D_MODEL // 2
H_RNN = D_RNN // HEAD_DIM
RNN_BLOCK = D_RNN // H_RNN
CONV_W = 4
RG_C = 8.0
H_FOX = (D_MODEL // 2) // HEAD_DIM
FOX_W = H_FOX * HEAD_DIM
H_SB = D_MODEL // HEAD_DIM
SB_W = H_SB * HEAD_DIM
D_FF = 11 * D_MODEL // 4
Q_BLOCK = 128
RMS_EPS = 1e-6
N_EVEN = (DEPTH + 1) // 2
N_ODD = DEPTH // 2
HY_IN = 2 * D_RNN + 3 * FOX_W + H_FOX
HY_MIX = D_RNN + FOX_W

kernel_name = "hybrid_rglru_fox_stickbreak_macaron"


def _rmsnorm(x, g):
    x32 = x.astype(jnp.float32)
    y = x32 * lax.rsqrt(jnp.mean(x32 * x32, axis=-1, keepdims=True) + RMS_EPS)
    return (y * g.astype(jnp.float32)).astype(x.dtype)


def _swiglu(h, w_in, w_out):
    a, b = jnp.split(h @ w_in, 2, axis=-1)
    return (jax.nn.silu(a) * b) @ w_out


def _lin_combine(left, right):
    a1, b1 = left
    a2, b2 = right
    return a1 * a2, a2 * b1 + b2


def _rglru_branch(u, gate, conv_w, conv_b, gate_w, gate_b, lam):
    B, S, _ = u.shape
    up = jnp.pad(u, ((0, 0), (CONV_W - 1, 0), (0, 0)))
    xc = conv_b + sum(conv_w[j] * up[:, j:j + S] for j in range(CONV_W))
    xh = xc.reshape(B, S, H_RNN, RNN_BLOCK)
    g = jnp.einsum('bshi,ghij->gbshj', xh, gate_w).reshape(2, B, S, D_RNN) + gate_b[:, None, None, :]
    r = jax.nn.sigmoid(g[0].astype(jnp.float32))
    i = jax.nn.sigmoid(g[1].astype(jnp.float32))
    log_a = -RG_C * r * jax.nn.softplus(-lam.astype(jnp.float32))
    a = jnp.exp(log_a)
    b = jnp.sqrt(-jnp.expm1(2.0 * log_a)) * (i * xc.astype(jnp.float32))
    _, h = lax.associative_scan(_lin_combine, (a, b), axis=1)
    return h.astype(u.dtype) * jax.nn.gelu(gate)


def _to_query_blocks(q):
    B, H, S, d = q.shape
    return q.reshape(B, H, S // Q_BLOCK, Q_BLOCK, d).transpose(2, 0, 1, 3, 4)


def _from_query_blocks(o):
    nb, B, H, QB, d = o.shape
    return o.transpose(1, 0, 3, 2, 4).reshape(B, nb * QB, H * d)


def _fox_attention(q, k, v, logf):
    B, H, S, d = q.shape
    c = jnp.cumsum(logf, axis=-1)
    nb = S // Q_BLOCK
    qb = _to_query_blocks(q)
    cb = c.reshape(B, H, nb, Q_BLOCK).transpose(2, 0, 1, 3)
    kpos = jnp.arange(S)
    scale = d ** -0.5

    def block(args):
        qi, ci, bi = args
        qpos = bi * Q_BLOCK + jnp.arange(Q_BLOCK)
        s = jnp.einsum('bhqd,bhkd->bhqk', qi, k).astype(jnp.float32) * scale
        s = s + ci[..., None] - c[:, :, None, :]
        s = jnp.where(kpos[None, :] <= qpos[:, None], s, -jnp.inf)
        p = jax.nn.softmax(s, axis=-1)
        return jnp.einsum('bhqk,bhkd->bhqd', p.astype(v.dtype), v)

    return _from_query_blocks(lax.map(block, (qb, cb, jnp.arange(nb))))


def _stick_breaking_attention(q, k, v):
    B, H, S, d = q.shape
    nb = S // Q_BLOCK
    qb = _to_query_blocks(q)
    kpos = jnp.arange(S)
    scale = d ** -0.5

    def block(args):
        qi, bi = args
        qpos = bi * Q_BLOCK + jnp.arange(Q_BLOCK)
        mask = kpos[None, :] < qpos[:, None]
        z = jnp.einsum('bhqd,bhkd->bhqk', qi, k).astype(jnp.float32) * scale
        log_nb = jnp.where(mask, jax.nn.log_sigmoid(-z), 0.0)
        after = lax.cumsum(log_nb, axis=3, reverse=True) - log_nb
        w = jnp.where(mask, jnp.exp(jax.nn.log_sigmoid(z) + after), 0.0)
        return jnp.einsum('bhqk,bhkd->bhqd', w.astype(v.dtype), v)

    return _from_query_blocks(lax.map(block, (qb, jnp.arange(nb))))


def _heads(t, n):
    B, S, _ = t.shape
    return t.reshape(B, S, n, HEAD_DIM)


def _hybrid_mixer(h, w_in, conv_w, conv_b, gate_w, gate_b, lam, f_b, qk_g, w_out):
    proj = h @ w_in
    o = 2 * D_RNN
    u, gate, q, k, v, f = jnp.split(proj, [D_RNN, o, o + FOX_W, o + 2 * FOX_W, o + 3 * FOX_W], axis=-1)
    y_rnn = _rglru_branch(u, gate, conv_w, conv_b, gate_w, gate_b, lam)
    qh = _rmsnorm(_heads(q, H_FOX), qk_g[0]).transpose(0, 2, 1, 3)
    kh = _rmsnorm(_heads(k, H_FOX), qk_g[1]).transpose(0, 2, 1, 3)
    vh = _heads(v, H_FOX).transpose(0, 2, 1, 3)
    logf = jax.nn.log_sigmoid((f + f_b).astype(jnp.float32)).transpose(0, 2, 1)
    y_fox = _fox_attention(qh, kh, vh, logf)
    return jnp.concatenate([y_rnn, y_fox], axis=-1) @ w_out


def _sb_mixer(h, w_qkv, w_out):
    q, k, v = jnp.split(h @ w_qkv, 3, axis=-1)
    qh, kh, vh = (_heads(t, H_SB).transpose(0, 2, 1, 3) for t in (q, k, v))
    return _stick_breaking_attention(qh, kh, vh) @ w_out


def setup_inputs(seed: int = 0) -> dict:
    key = jax.random.key(seed)
    ks = jax.random.split(key, 20)
    f32 = jnp.float32

    def nrm(k, shape, fan_in):
        return jax.random.normal(k, shape, f32) * (fan_in ** -0.5)

    a0 = jax.random.uniform(ks[9], (N_EVEN, D_RNN), f32, 0.9, 0.999)
    return {
        "x": jax.random.normal(ks[0], (BATCH, SEQ, D_MODEL), f32),
        "ffn_norm": 1.0 + 0.05 * jax.random.normal(ks[1], (DEPTH, 2, D_MODEL), f32),
        "ffn_w_in": nrm(ks[2], (DEPTH, 2, D_MODEL, 2 * D_FF), D_MODEL),
        "ffn_w_out": nrm(ks[3], (DEPTH, 2, D_FF, D_MODEL), D_FF),
        "mix_norm": 1.0 + 0.05 * jax.random.normal(ks[4], (DEPTH, D_MODEL), f32),
        "hy_w_in": nrm(ks[5], (N_EVEN, D_MODEL, HY_IN), D_MODEL),
        "rg_conv_w": nrm(ks[6], (N_EVEN, CONV_W, D_RNN), CONV_W),
        "rg_conv_b": 0.02 * jax.random.normal(ks[7], (N_EVEN, D_RNN), f32),
        "rg_gate_w": nrm(ks[8], (N_EVEN, 2, H_RNN, RNN_BLOCK, RNN_BLOCK), RNN_BLOCK),
        "rg_gate_b": 0.02 * jax.random.normal(ks[10], (N_EVEN, 2, D_RNN), f32),
        "rg_lambda": jnp.log(a0) - jnp.log1p(-a0),
        "fox_fgate_b": jax.random.uniform(ks[11], (N_EVEN, H_FOX), f32, 1.0, 4.0),
        "fox_qk_norm": 1.0 + 0.05 * jax.random.normal(ks[12], (N_EVEN, 2, HEAD_DIM), f32),
        "hy_w_out": nrm(ks[13], (N_EVEN, HY_MIX, D_MODEL), HY_MIX),
        "sb_w_qkv": nrm(ks[14], (N_ODD, D_MODEL, 3 * SB_W), D_MODEL),
        "sb_w_out": nrm(ks[15], (N_ODD, SB_W, D_MODEL), SB_W),
    }


def reference(x, ffn_norm, ffn_w_in, ffn_w_out, mix_norm, hy_w_in, rg_conv_w, rg_conv_b,
              rg_gate_w, rg_gate_b, rg_lambda, fox_fgate_b, fox_qk_norm, hy_w_out,
              sb_w_qkv, sb_w_out):
    for layer in range(DEPTH):
        x = x + 0.5 * _swiglu(_rmsnorm(x, ffn_norm[layer, 0]), ffn_w_in[layer, 0], ffn_w_out[layer, 0])
        h = _rmsnorm(x, mix_norm[layer])
        if layer % 2 == 0:
            e = layer // 2
            y = _hybrid_mixer(h, hy_w_in[e], rg_conv_w[e], rg_conv_b[e], rg_gate_w[e], rg_gate_b[e],
                              rg_lambda[e], fox_fgate_b[e], fox_qk_norm[e], hy_w_out[e])
        else:
            o = layer // 2
            y = _sb_mixer(h, sb_w_qkv[o], sb_w_out[o])
        x = x + y
        x = x + 0.5 * _swiglu(_rmsnorm(x, ffn_norm[layer, 1]), ffn_w_in[layer, 1], ffn_w_out[layer, 1])
    return x
```

```python
import numpy as np
import ml_dtypes
from contextlib import ExitStack
import concourse.bass as bass
import concourse.mybir as mybir
from concourse.bass_utils import run_bass_kernel_spmd

F32 = mybir.dt.float32
BF16 = mybir.dt.bfloat16
AF = mybir.ActivationFunctionType
ALU = mybir.AluOpType

D = 1024
DFF = 2816
NCH = 22
RMS_EPS = 1e-6
MASKV = -30000.0
ENGS = ["pe", "act", "dve", "pool", "sp"]
BLK = {"pe": "tensor", "act": "scalar", "dve": "vector", "pool": "gpsimd", "sp": "sync"}


class Cfg:
    def __init__(self, S=8192, depth=4, npairs=4, cc_bytes=2 * 1024 * 1024):
        self.S = S
        self.TL = S // 2
        self.NG = self.TL // 512
        self.NGG = S // 512
        self.NB = S // 128
        self.depth = depth
        self.npairs = npairs
        self.n_even = (depth + 1) // 2
        self.n_odd = depth // 2
        self.cc_bytes = cc_bytes
        c = 0
        self.c_ffn = c; c += depth * 2 * 8
        self.c_mix = c; c += depth * 8
        self.c_qk = c; c += max(1, self.n_even) * 2
        self.c_rg = c; c += max(1, self.n_even) * 4 * 8
        self.c_fb = c; c += max(1, self.n_even)
        self.c_sel = c; c += 2
        self.NPP = c


class Buf:
    __slots__ = ("w", "r", "mw")

    def __init__(self):
        self.w = None
        self.r = {}
        self.mw = {}


class DSem:
    def __init__(self, key):
        self.key = key
        self.val = 0


class Sched:
    EPOCH = 28000

    def __init__(self, nc, stack):
        self.nc = nc
        self.stack = stack
        self.sems = []
        self.ops = {e: [] for e in ENGS}
        self.cnt = {e: 0 for e in ENGS}
        self.nep = {e: 0 for e in ENGS}
        self.cur = {e: self._new(f"{e}_0") for e in ENGS}
        self.seen = {e: {} for e in ENGS}
        self.pend = {e: [] for e in ENGS}
        self.dsems = []
        self.dsem_names = {}
        self.final = {}

    def _new(self, name):
        s = self.stack.enter_context(self.nc.semaphore(name))
        self.sems.append(s)
        return len(self.sems) - 1

    def dsem(self, name):
        if name in self.dsem_names:
            return self.dsem_names[name]
        d = DSem(self._new(name))
        self.dsems.append(d)
        self.dsem_names[name] = d
        return d

    def op(self, e, fn, reads=(), writes=(), mwrites=(), dsem=None, inc=16):
        waits = {}

        def need(k, v):
            if waits.get(k, 0) < v:
                waits[k] = v

        for b in reads:
            if b.w is not None:
                need(*b.w)
            for k, v in b.mw.items():
                need(k, v)
        for b in writes:
            if b.w is not None:
                need(*b.w)
            for k, v in b.r.items():
                need(k, v)
            for k, v in b.mw.items():
                need(k, v)
        for b in mwrites:
            if b.w is not None:
                need(*b.w)
            for k, v in b.r.items():
                need(k, v)
        for k, v in self.pend[e]:
            need(k, v)
        self.pend[e] = []
        seen = self.seen[e]
        wl = []
        own = self.cur[e]
        for k, v in waits.items():
            if seen.get(k, 0) >= v:
                continue
            seen[k] = v
            if e == "pe" and k == own:
                continue
            wl.append((k, v))
        if dsem is None:
            if self.cnt[e] >= self.EPOCH:
                self.nep[e] += 1
                self.cur[e] = self._new(f"{e}_{self.nep[e]}")
                self.cnt[e] = 0
            self.cnt[e] += 1
            h = (self.cur[e], self.cnt[e])
            incv = 1
        else:
            dsem.val += inc
            h = (dsem.key, dsem.val)
            incv = inc
        for b in writes:
            b.w = h
            b.r = {}
            b.mw = {}
        for b in mwrites:
            if b.mw.get(h[0], 0) < h[1]:
                b.mw[h[0]] = h[1]
        for b in reads:
            if b.r.get(h[0], 0) < h[1]:
                b.r[h[0]] = h[1]
        self.ops[e].append((wl, fn, (h[0], incv)))
        self.final[h[0]] = max(self.final.get(h[0], 0), h[1])
        return h

    def barrier(self):
        hs = [(self.cur[e], self.cnt[e]) for e in ENGS if self.cnt[e] > 0]
        hs += [(d.key, d.val) for d in self.dsems if d.val > 0]
        for e in ENGS:
            self.pend[e] = list(hs)

    def emit(self):
        with self.nc.Block() as block:
            for e in ENGS:
                ops = self.ops[e]
                if not ops:
                    continue

                def body(engine, ops=ops):
                    for wl, fn, (k, incv) in ops:
                        for wk, wv in wl:
                            engine.wait_ge(self.sems[wk], wv)
                        ins = fn(engine)
                        ins.then_inc(self.sems[k], incv)

                getattr(block, BLK[e])(body)
        self.ops = {e: [] for e in ENGS}

    def finish(self):
        with self.nc.Block() as block:
            fin = dict(self.final)

            def body(engine):
                for k, v in fin.items():
                    engine.wait_ge(self.sems[k], v)

            block.sync(body)


class Slot:
    def __init__(self, S, ap, name):
        self.ap = ap
        self.buf = Buf()
        self.sem = S.dsem(name)


def build_program(cfg):
    nc = bass.Bass("TRN2", target_bir_lowering=False)
    S_, TL, NG, NGG, NB, depth = cfg.S, cfg.TL, cfg.NG, cfg.NGG, cfg.NB, cfg.depth
    ne, no = max(1, cfg.n_even), max(1, cfg.n_odd)

    def din(name, shape, dt):
        return nc.dram_tensor(name, list(shape), dt, kind="ExternalInput")

    x_in = din("x", [TL, D], F32)
    w_in_f = din("w_in", [depth, 2, 6, 128, 8 * 2 * 512], F32)
    w_out_f = din("w_out", [depth, 2, 8, 128, NCH * 128], F32)
    w_mi_f = din("w_mi", [depth, 3, 128, 8 * 2 * 512], F32)
    w_mo_f = din("w_mo", [depth, 128, 8 * 8 * 128], F32)
    gw_f = din("gate_w", [ne, 128, 2 * 4 * 128], F32)
    pp_d = din("pp", [128, cfg.NPP], F32)
    cb_d = din("cb", [128, 640], BF16)
    mk_d = din("mk", [128, 4 * 4 * 512], BF16)
    cf_d = din("cf", [128, 256], F32)
    out_d = nc.dram_tensor("out", [TL, D], F32, kind="ExternalOutput")

    w_in_b = nc.dram_tensor("w_in_b", [depth, 2, 6, 128, 8 * 2 * 512], BF16)
    w_out_b = nc.dram_tensor("w_out_b", [depth, 2, 8, 128, NCH * 128], BF16)
    w_mi_b = nc.dram_tensor("w_mi_b", [depth, 3, 128, 8 * 2 * 512], BF16)
    w_mo_b = nc.dram_tensor("w_mo_b", [depth, 128, 8 * 8 * 128], BF16)
    gw_b = nc.dram_tensor("gw_b", [ne, 128, 2 * 4 * 128], BF16)
    xT_d = nc.dram_tensor("xT", [8, 128, TL], F32)
    qT_d = nc.dram_tensor("qT", [1024, TL], BF16)
    y_d = nc.dram_tensor("yT", [1024, TL], BF16)
    gate_d = nc.dram_tensor("gateT", [512, TL], F32)
    CCB = cfg.cc_bytes

    class XB:
        def __init__(self, name, rows, dt, esz, cols=None):
            cols = cols or TL
            self.CR = min(rows, max(1, CCB // (cols * esz)))
            assert rows % self.CR == 0
            self.n = rows // self.CR
            self.snd = [nc.dram_tensor(f"{name}_s{c}", [self.CR, cols], dt) for c in range(self.n)]
            self.rcv = [nc.dram_tensor(f"{name}_r{c}", [2 * self.CR, cols], dt) for c in range(self.n)]

        def srows(self, r0, n):
            c, o = divmod(r0, self.CR)
            assert o + n <= self.CR, (r0, n, self.CR)
            return self.snd[c][o:o + n, :]

        def rrows(self, rank, r0, n):
            c, o = divmod(r0, self.CR)
            assert o + n <= self.CR
            return self.rcv[c][rank * self.CR + o:rank * self.CR + o + n, :]

    xb_sb = XB("xsb", 2048, BF16, 2)
    xb_hb = XB("xhb", 1024, BF16, 2)
    xb_f = XB("xf", 512, F32, 4)
    snd_l = nc.dram_tensor("snd_l", [512, TL // 64], F32)
    rcv_l = nc.dram_tensor("rcv_l", [1024, TL // 64], F32)
    snd_lv = snd_l.ap().rearrange("(h a) b -> h (a b)", h=8)
    rcv_lv = rcv_l.ap().rearrange("(r h a) b -> r h (a b)", r=2, h=8)
    RG = [[2 * i, 2 * i + 1] for i in range(cfg.npairs)]

    gstack = ExitStack()
    with gstack:
        S = Sched(nc, gstack)
        b_win = [[[Buf() for _ in range(6)] for _ in range(2)] for _ in range(depth)]
        b_wout = [[[Buf() for _ in range(8)] for _ in range(2)] for _ in range(depth)]
        b_wmi = [[Buf() for _ in range(3)] for _ in range(depth)]
        b_wmo = [Buf() for _ in range(depth)]
        b_gw = [Buf() for _ in range(ne)]
        b_xT = [Buf() for _ in range(NG)]
        b_y = [Buf() for _ in range(NG)]
        b_qT = Buf()
        b_gate = Buf()
        b_sndb, b_rcvb, b_sndf, b_rcvf = Buf(), Buf(), Buf(), Buf()
        b_out = Buf()
        cvt_sems = [S.dsem(f"cvt{i}") for i in range(8)]
        cvt_n = [0]
        cc_sem = S.dsem("cc")
        uid = [0]

        def cvt(dst, src, buf):
            ds_ = cvt_sems[cvt_n[0] % len(cvt_sems)]
            cvt_n[0] += 1
            n = 1
            for s_ in src.shape:
                n *= s_
            assert n % 2048 == 0 or n < 2048 * 16, n
            S.op("pool", lambda e, dst=dst, src=src: e.dma_start(out=dst, in_=src), writes=[buf], dsem=ds_)

        def flat2(ap, cols=2048):
            sh = ap.shape
            if len(sh) == 2:
                return ap.rearrange("a (b c) -> (a b) c", c=cols)
            return ap.rearrange("m a (b c) -> (m a b) c", c=cols)

        def cvt_ffn(l, i):
            for s_ in range(6):
                cvt(flat2(w_in_b[l, i, s_]), flat2(w_in_f[l, i, s_]), b_win[l][i][s_])
            for m in range(8):
                cvt(flat2(w_out_b[l, i, m], 1408), flat2(w_out_f[l, i, m], 1408), b_wout[l][i][m])

        def cvt_mix(l):
            for s_ in range(3):
                cvt(flat2(w_mi_b[l, s_]), flat2(w_mi_f[l, s_]), b_wmi[l][s_])
            cvt(flat2(w_mo_b[l]), flat2(w_mo_f[l]), b_wmo[l])
            if l % 2 == 0:
                cvt(gw_b[l // 2], gw_f[l // 2], b_gw[l // 2])

        def token_pass(kind, l):
            with ExitStack() as ps:
                uid[0] += 1
                u_ = uid[0]

                def sb(name, shape, dt, u_=u_):
                    return ps.enter_context(nc.sbuf_tensor(f"s{u_}_{name}", list(shape), dt))

                def pst(name, shape=(128, 512), dt=F32, u_=u_):
                    return ps.enter_context(nc.psum_tensor(f"p{u_}_{name}", list(shape), dt))

                S.barrier()
                pp = sb("pp", [128, cfg.NPP], F32)
                cb = sb("cb", [128, 640], BF16)
                cf = sb("cf", [128, 256], F32)
                ident_f = cf[:, 0:128]
                ones_b = cb[:, 128:256]
                bd_b = cb[:, 512:640]
                b_const = Buf()
                cs = S.dsem("tp_const")
                S.op("sp", lambda e: e.dma_start(out=pp[:], in_=pp_d[:, :]), writes=[b_const], dsem=cs)
                S.op("sp", lambda e: e.dma_start(out=cb[:], in_=cb_d[:, :]), mwrites=[b_const], dsem=cs)
                S.op("sp", lambda e: e.dma_start(out=cf[:], in_=cf_d[:, :]), mwrites=[b_const], dsem=cs)
                nfb = sb("nfb", [128, ne], F32)
                S.op("dve", lambda e: e.tensor_scalar(out=nfb[:], in0=pp[:, cfg.c_fb:cfg.c_fb + ne], scalar1=-1.0,
                                                      scalar2=None, op0=ALU.mult), reads=[b_const], writes=[b_const])

                xT = sb("xT", [128, 8, 512], F32)
                bx = Buf()
                xsem = S.dsem("xT")
                hT = sb("hT", [128, 8, 512], BF16)
                bh = Buf()
                gT = sb("gT", [128, NCH, 512], BF16)
                bg = Buf()
                sq = gT[:, 0:8, :]
                rs_t = sb("rs_t", [128, 512], F32)
                rstd = sb("rstd", [128, 512], F32)
                brs = Buf()
                sa = [sb(f"sa{i}", [128, 512], F32) for i in range(2)]
                bsa = [Buf() for _ in range(2)]
                wins = [Slot(S, sb(f"win{i}", [128, 8, 2, 512], BF16), f"win{i}") for i in range(2)]
                wouts = [Slot(S, sb(f"wout{i}", [128, NCH, 128], BF16), f"wout{i}") for i in range(3)]
                nwin = [0]
                nwout = [0]
                ps_ss = pst("ps_ss")
                bss = Buf()
                ps_a = [pst(f"ps_a{i}") for i in range(2)]
                ps_b = [pst(f"ps_b{i}") for i in range(2)]
                ps_o = [pst(f"ps_o{i}") for i in range(2)]
                bpa = [Buf() for _ in range(2)]
                bpb = [Buf() for _ in range(2)]
                bpo = [Buf() for _ in range(2)]
                npo = [0]
                if kind != "first":
                    wo = sb("wo", [128, 8, 8, 128], BF16)
                    bwo = Buf()
                    S.op("sp", lambda e: e.dma_start(out=wo[:].rearrange("p a b c -> p (a b c)"), in_=w_mo_b[l]),
                         reads=[b_wmo[l]], writes=[bwo], dsem=S.dsem("wo"))
                    yTs = [Slot(S, sb(f"yT{i}", [128, 8, 512], BF16), f"yT{i}") for i in range(2)]
                if kind in ("first", "last"):
                    xtok = sb("xtok", [128, 4, 1024], F32)
                    bxt = Buf()
                    xtsem = S.dsem("xtok")
                if kind != "last":
                    ln = l if kind == "first" else l + 1
                    hyb = (ln % 2 == 0)
                    stA = [sb(f"stA{i}", [128, 8, 512], BF16) for i in range(2)]
                    bstA = [Buf() for _ in range(2)]
                    stAsem = [S.dsem(f"stA{i}") for i in range(2)]
                    vst = sb("vst", [128, 16, 4, 64], BF16)
                    bvst = Buf()
                    vsem = S.dsem("vst")
                    if hyb:
                        stF = sb("stF", [128, 8, 512], F32)
                        bstF = Buf()
                        stFsem = S.dsem("stF")
                        fst = sb("fst", [8, 3, 512], F32)
                        bfst = Buf()
                        fsem = S.dsem("fst")
                        qn_t = sb("qn_t", [128, 512], F32)
                        qn_r = sb("qn_r", [128, 512], F32)
                        bqn = Buf()
                        sqq = sb("sqq", [128, 512], BF16)
                        bsqq = Buf()

                def load_win(src_ap, src_buf, ncols=512):
                    sl = wins[nwin[0] % len(wins)]
                    nwin[0] += 1
                    if ncols == 512:
                        S.op("sp", lambda e, sl=sl: e.dma_start(out=sl.ap[:].rearrange("p a b c -> p (a b c)"), in_=src_ap),
                             reads=[src_buf], writes=[sl.buf], dsem=sl.sem)
                    else:
                        S.op("sp", lambda e, sl=sl: e.dma_start(
                            out=sl.ap[:, :, :, 0:ncols],
                            in_=src_ap.rearrange("p (a b c) -> p a b c", a=8, b=2)[:, :, :, 0:ncols]),
                            reads=[src_buf], writes=[sl.buf], dsem=sl.sem)
                    return sl

                def load_wout(src_ap, src_buf):
                    sl = wouts[nwout[0] % len(wouts)]
                    nwout[0] += 1
                    S.op("sp", lambda e, sl=sl: e.dma_start(out=sl.ap[:].rearrange("p a c -> p (a c)"), in_=src_ap),
                         reads=[src_buf], writes=[sl.buf], dsem=sl.sem)
                    return sl

                def rmsnorm(gcol):
                    S.op("act", lambda e: e.activation(out=sq.rearrange("p a c -> p (a c)"),
                                                       in_=xT[:].rearrange("p a c -> p (a c)"), func=AF.Square),
                         reads=[bx], writes=[bg])
                    for k in range(8):
                        S.op("pe", lambda e, k=k: e.matmul(ps_ss[:], ones_b, sq[:, k, :], start=(k == 0), stop=(k == 7)),
                             reads=[bg, b_const], writes=[bss])
                    S.op("act", lambda e: e.activation(out=rs_t[:], in_=ps_ss[:], func=AF.Sqrt, bias=RMS_EPS, scale=1.0 / D),
                         reads=[bss], writes=[brs])
                    S.op("dve", lambda e: e.reciprocal(out=rstd[:], in_=rs_t[:]), reads=[brs], writes=[brs])
                    for k in range(8):
                        S.op("dve", lambda e, k=k: e.scalar_tensor_tensor(
                            out=hT[:, k, :], in0=xT[:, k, :], scalar=pp[:, gcol + k:gcol + k + 1], in1=rstd[:],
                            op0=ALU.mult, op1=ALU.mult), reads=[bx, brs, b_const], writes=[bh])

                def ffn(lf, i):
                    rmsnorm(cfg.c_ffn + (lf * 2 + i) * 8)
                    cnt = 0
                    for s_ in range(6):
                        ncols = 512 if s_ < 5 else 256
                        sl = load_win(w_in_b[lf, i, s_], b_win[lf][i][s_], ncols)
                        for j in range(ncols // 128):
                            c = s_ * 4 + j
                            pa, pb = ps_a[cnt % 2], ps_b[cnt % 2]
                            ba, bb = bpa[cnt % 2], bpb[cnt % 2]
                            sat, bsat = sa[cnt % 2], bsa[cnt % 2]
                            cnt += 1
                            for k in range(8):
                                S.op("pe", lambda e, sl=sl, k=k, j=j, pa=pa: e.matmul(
                                    pa[:], sl.ap[:, k, 0, j * 128:(j + 1) * 128], hT[:, k, :], start=(k == 0), stop=(k == 7)),
                                    reads=[sl.buf, bh], writes=[ba])
                            for k in range(8):
                                S.op("pe", lambda e, sl=sl, k=k, j=j, pb=pb: e.matmul(
                                    pb[:], sl.ap[:, k, 1, j * 128:(j + 1) * 128], hT[:, k, :], start=(k == 0), stop=(k == 7)),
                                    reads=[sl.buf, bh], writes=[bb])
                            S.op("act", lambda e, pa=pa, sat=sat: e.activation(out=sat[:], in_=pa[:], func=AF.Silu),
                                 reads=[ba], writes=[bsat])
                            S.op("dve", lambda e, c=c, pb=pb, sat=sat: e.tensor_tensor(
                                out=gT[:, c, :], in0=sat[:], in1=pb[:], op=ALU.mult), reads=[bsat, bb], writes=[bg])
                    for m in range(8):
                        sl = load_wout(w_out_b[lf, i, m], b_wout[lf][i][m])
                        po, bo = ps_o[npo[0] % 2], bpo[npo[0] % 2]
                        npo[0] += 1
                        for f in range(NCH):
                            S.op("pe", lambda e, sl=sl, f=f, po=po: e.matmul(
                                po[:], sl.ap[:, f, :], gT[:, f, :], start=(f == 0), stop=(f == NCH - 1)),
                                reads=[sl.buf, bg], writes=[bo])
                        S.op("dve", lambda e, m=m, po=po: e.scalar_tensor_tensor(
                            out=xT[:, m, :], in0=po[:], scalar=0.5, in1=xT[:, m, :], op0=ALU.mult, op1=ALU.add),
                            reads=[bo, bx], writes=[bx])

                def store(eng, dst, src, rbufs, mw, dsem):
                    S.op(eng, lambda e: e.dma_start(out=dst, in_=src), reads=rbufs, mwrites=mw, dsem=dsem)

                def xstore_rows(xb, r0, st, nchunk, t0, bst, ssem, mw=None):
                    per = max(1, min(nchunk, xb.CR // 128))
                    for c0 in range(0, nchunk, per):
                        dst = xb.srows(r0 + c0 * 128, per * 128)[:, t0:t0 + 512].rearrange("(c p) t -> p c t", p=128)
                        store("act", dst, st[:, c0:c0 + per, :], [bst], [b_sndb if mw is None else mw], ssem)

                def xstore_v(xb, voff_, nh, tt):
                    per = max(1, min(nh, xb.CR // 64))
                    for h0 in range(0, nh, per):
                        vv = xb.srows(voff_ + h0 * 64, per * 64).rearrange("a (two c) -> (a two) c", two=2)
                        vv = vv.rearrange("(h p) (b d) -> p h b d", p=128, d=64)
                        store("act", vv[:, :, tt * 4:(tt + 1) * 4, :], vst[:, h0:h0 + per, :, :], [bvst], [b_sndb], vsem)

                def proj_sb(tt):
                    t0 = tt * 512
                    rmsnorm(cfg.c_mix + ln * 8)
                    for which in range(2):
                        sl = load_win(w_mi_b[ln, which], b_wmi[ln][which])
                        st, bst, ssem = stA[which], bstA[which], stAsem[which]
                        for c in range(8):
                            pa, ba = ps_a[c % 2], bpa[c % 2]
                            for k in range(8):
                                S.op("pe", lambda e, sl=sl, k=k, c=c, pa=pa: e.matmul(
                                    pa[:], sl.ap[:, k, c // 4, (c % 4) * 128:(c % 4 + 1) * 128], hT[:, k, :],
                                    start=(k == 0), stop=(k == 7)), reads=[sl.buf, bh], writes=[ba])
                            S.op("act", lambda e, c=c, pa=pa, st=st: e.activation(out=st[:, c, :], in_=pa[:], func=AF.Copy),
                                 reads=[ba], writes=[bst])
                        if which == 0:
                            dst = qT_d[:, t0:t0 + 512].rearrange("(c p) t -> p c t", p=128)
                            store("act", dst, st[:], [bst], [b_qT], ssem)
                        else:
                            xstore_rows(xb_sb, 0, st, 8, t0, bst, ssem)
                    sl = load_win(w_mi_b[ln, 2], b_wmi[ln][2])
                    n = 0
                    for j in range(4):
                        for half in range(2):
                            pb, bb = ps_b[n % 2], bpb[n % 2]
                            n += 1
                            for k in range(8):
                                S.op("pe", lambda e, sl=sl, k=k, j=j, half=half, pb=pb: e.matmul(
                                    pb[:], hT[:, k, j * 128:(j + 1) * 128], sl.ap[:, k, half, :],
                                    start=(k == 0), stop=(k == 7)), reads=[sl.buf, bh], writes=[bb])
                            S.op("dve", lambda e, j=j, half=half, pb=pb: e.tensor_copy(
                                out=vst[:, half * 8:(half + 1) * 8, j, :], in_=pb[:].rearrange("p (h d) -> p h d", d=64)),
                                reads=[bb], writes=[bvst])
                    xstore_v(xb_sb, 1024, 16, tt)

                def proj_hy(tt):
                    t0 = tt * 512
                    e_ = ln // 2
                    rmsnorm(cfg.c_mix + ln * 8)
                    sl = load_win(w_mi_b[ln, 0], b_wmi[ln][0])
                    for c in range(8):
                        pa, ba = ps_a[c % 2], bpa[c % 2]
                        for k in range(8):
                            S.op("pe", lambda e, sl=sl, k=k, c=c, pa=pa: e.matmul(
                                pa[:], sl.ap[:, k, c // 4, (c % 4) * 128:(c % 4 + 1) * 128], hT[:, k, :],
                                start=(k == 0), stop=(k == 7)), reads=[sl.buf, bh], writes=[ba])
                        S.op("act", lambda e, c=c, pa=pa: e.activation(out=stF[:, c, :], in_=pa[:], func=AF.Copy),
                             reads=[ba], writes=[bstF])
                    xstore_rows(xb_f, 0, stF, 4, t0, bstF, stFsem, mw=b_sndf)
                    store("act", gate_d[:, t0:t0 + 512].rearrange("(c p) t -> p c t", p=128), stF[:, 4:8, :],
                          [bstF], [b_gate], stFsem)
                    sl = load_win(w_mi_b[ln, 1], b_wmi[ln][1])
                    for c in range(8):
                        which = c // 4
                        pa, ba = ps_a[c % 2], bpa[c % 2]
                        for k in range(8):
                            S.op("pe", lambda e, sl=sl, k=k, c=c, pa=pa: e.matmul(
                                pa[:], sl.ap[:, k, c // 4, (c % 4) * 128:(c % 4 + 1) * 128], hT[:, k, :],
                                start=(k == 0), stop=(k == 7)), reads=[sl.buf, bh], writes=[ba])
                        S.op("act", lambda e, pa=pa: e.activation(out=sqq[:], in_=pa[:], func=AF.Square),
                             reads=[ba], writes=[bsqq])
                        S.op("pe", lambda e: e.matmul(ps_ss[:], bd_b, sqq[:], start=True, stop=True),
                             reads=[bsqq, b_const], writes=[bss])
                        S.op("act", lambda e: e.activation(out=qn_t[:], in_=ps_ss[:], func=AF.Sqrt, bias=RMS_EPS, scale=1.0 / 64),
                             reads=[bss], writes=[bqn])
                        S.op("dve", lambda e: e.reciprocal(out=qn_r[:], in_=qn_t[:]), reads=[bqn], writes=[bqn])
                        gc = cfg.c_qk + e_ * 2 + which
                        S.op("dve", lambda e, c=c, pa=pa, gc=gc: e.scalar_tensor_tensor(
                            out=stA[c // 4][:, c % 4, :], in0=pa[:], scalar=pp[:, gc:gc + 1], in1=qn_r[:],
                            op0=ALU.mult, op1=ALU.mult), reads=[ba, bqn, b_const], writes=[bstA[c // 4]])
                        if c == 3:
                            store("act", qT_d[0:512, t0:t0 + 512].rearrange("(c p) t -> p c t", p=128), stA[0][:, 0:4, :],
                                  [bstA[0]], [b_qT], stAsem[0])
                        if c == 7:
                            xstore_rows(xb_hb, 0, stA[1], 4, t0, bstA[1], stAsem[1])
                    sl = load_win(w_mi_b[ln, 2], b_wmi[ln][2])
                    for j in range(4):
                        pb, bb = ps_b[j % 2], bpb[j % 2]
                        for k in range(8):
                            S.op("pe", lambda e, sl=sl, k=k, j=j, pb=pb: e.matmul(
                                pb[:], hT[:, k, j * 128:(j + 1) * 128], sl.ap[:, k, 0, :],
                                start=(k == 0), stop=(k == 7)), reads=[sl.buf, bh], writes=[bb])
                        S.op("dve", lambda e, j=j, pb=pb: e.tensor_copy(
                            out=vst[:, 0:8, j, :], in_=pb[:].rearrange("p (h d) -> p h d", d=64)),
                            reads=[bb], writes=[bvst])
                    xstore_v(xb_hb, 512, 8, tt)
                    pa, ba = ps_a[0], bpa[0]
                    for k in range(8):
                        S.op("pe", lambda e, sl=sl, k=k, pa=pa: e.matmul(
                            pa[0:8, :], sl.ap[:, k, 1, 0:8], hT[:, k, :], start=(k == 0), stop=(k == 7)),
                            reads=[sl.buf, bh], writes=[ba])
                    S.op("act", lambda e, pa=pa: e.activation(out=fst[:, 0, :], in_=pa[0:8, :], func=AF.Exp,
                                                              bias=nfb[0:8, e_:e_ + 1], scale=-1.0),
                         reads=[ba, b_const], writes=[bfst])
                    S.op("act", lambda e: e.activation(out=fst[:, 1, :], in_=fst[:, 0, :], func=AF.Ln, bias=1.0),
                         reads=[bfst], writes=[bfst])
                    S.op("dve", lambda e: e.tensor_scalar(out=fst[:, 2, :], in0=fst[:, 1, :], scalar1=-1.0, scalar2=None,
                                                          op0=ALU.mult), reads=[bfst], writes=[bfst])
                    store("act", snd_lv[:, t0:t0 + 512], fst[:, 2, :], [bfst], [b_sndf], fsem)

                for tt in range(NG):
                    t0 = tt * 512
                    if kind == "first":
                        S.op("sp", lambda e, t0=t0: e.dma_start(
                            out=xtok[:], in_=x_in[t0:t0 + 512, :].rearrange("(j p) f -> p j f", p=128)),
                            writes=[bxt], dsem=xtsem)
                        for k in range(8):
                            po, bo = ps_o[npo[0] % 2], bpo[npo[0] % 2]
                            npo[0] += 1
                            for j in range(4):
                                S.op("pe", lambda e, k=k, j=j, po=po: e.transpose(
                                    po[:, j * 128:(j + 1) * 128], xtok[:, j, k * 128:(k + 1) * 128], ident_f),
                                    reads=[bxt, b_const], writes=[bo])
                            S.op("dve", lambda e, k=k, po=po: e.tensor_copy(out=xT[:, k, :], in_=po[:]),
                                 reads=[bo], writes=[bx])
                        ffn(0, 0)
                    else:
                        ysl = yTs[tt % 2]
                        S.op("sp", lambda e, t0=t0, ysl=ysl: e.dma_start(
                            out=ysl.ap[:], in_=y_d[:, t0:t0 + 512].rearrange("(c p) t -> p c t", p=128)),
                            reads=[b_y[tt]], writes=[ysl.buf], dsem=ysl.sem)
                        S.op("sp", lambda e, t0=t0: e.dma_start(
                            out=xT[:], in_=xT_d[:, :, t0:t0 + 512].rearrange("c p t -> p c t")),
                            reads=[b_xT[tt]], writes=[bx], dsem=xsem)
                        for m in range(8):
                            po, bo = ps_o[npo[0] % 2], bpo[npo[0] % 2]
                            npo[0] += 1
                            for f in range(8):
                                S.op("pe", lambda e, m=m, f=f, po=po, ysl=ysl: e.matmul(
                                    po[:], wo[:, m, f, :], ysl.ap[:, f, :], start=(f == 0), stop=(f == 7)),
                                    reads=[bwo, ysl.buf], writes=[bo])
                            S.op("dve", lambda e, m=m, po=po: e.tensor_tensor(
                                out=xT[:, m, :], in0=po[:], in1=xT[:, m, :], op=ALU.add), reads=[bo, bx], writes=[bx])
                        ffn(l, 1)
                        if kind == "mid":
                            ffn(l + 1, 0)
                    if kind != "last":
                        S.op("act", lambda e, t0=t0: e.dma_start(
                            out=xT_d[:, :, t0:t0 + 512].rearrange("c p t -> p c t"), in_=xT[:]),
                            reads=[bx], writes=[b_xT[tt]], dsem=xsem)
                        if hyb:
                            proj_hy(tt)
                        else:
                            proj_sb(tt)
                    else:
                        for j in range(4):
                            for kk in range(2):
                                po, bo = ps_o[npo[0] % 2], bpo[npo[0] % 2]
                                npo[0] += 1
                                for k4 in range(4):
                                    k = kk * 4 + k4
                                    S.op("pe", lambda e, k=k, k4=k4, j=j, po=po: e.transpose(
                                        po[:, k4 * 128:(k4 + 1) * 128], xT[:, k, j * 128:(j + 1) * 128], ident_f),
                                        reads=[bx, b_const], writes=[bo])
                                S.op("dve", lambda e, kk=kk, j=j, po=po: e.tensor_copy(
                                    out=xtok[:, j, kk * 512:(kk + 1) * 512], in_=po[:]), reads=[bo], writes=[bxt])
                        S.op("act", lambda e, t0=t0: e.dma_start(
                            out=out_d[t0:t0 + 512, :].rearrange("(j p) f -> p j f", p=128), in_=xtok[:]),
                            reads=[bxt], mwrites=[b_out], dsem=xtsem)
                S.emit()

        def exchange(hyb):
            S.barrier()

            def ag(src, dst, rb, wb, first):
                S.op("pool", lambda e: e.collective_compute("AllGather", ALU.bypass, replica_groups=RG,
                                                            ins=[src.ap().opt()], outs=[dst.ap().opt()]),
                     reads=[rb], **({"writes": [wb]} if first else {"mwrites": [wb]}), dsem=cc_sem, inc=1)

            xb = xb_hb if hyb else xb_sb
            for c in range(xb.n):
                ag(xb.snd[c], xb.rcv[c], b_sndb, b_rcvb, c == 0)
            if hyb:
                for c in range(xb_f.n):
                    ag(xb_f.snd[c], xb_f.rcv[c], b_sndf, b_rcvf, c == 0)
                ag(snd_l, rcv_l, b_sndf, b_rcvf, False)
            S.emit()

        def attn_phase(l):
            hyb = (l % 2 == 0)
            e_ = l // 2
            with ExitStack() as ps:
                uid[0] += 1
                u_ = uid[0]

                def sb(name, shape, dt, u_=u_):
                    return ps.enter_context(nc.sbuf_tensor(f"s{u_}_{name}", list(shape), dt))

                def pst(name, shape=(128, 512), dt=F32, u_=u_):
                    return ps.enter_context(nc.psum_tensor(f"p{u_}_{name}", list(shape), dt))

                S.barrier()
                pp = sb("pp", [128, cfg.NPP], F32)
                cb = sb("cb", [128, 640], BF16)
                cf = sb("cf", [128, 256], F32)
                mk = sb("mk", [128, 4, 4, 512], BF16)
                b_const = Buf()
                cs = S.dsem("at_const")
                S.op("sp", lambda e: e.dma_start(out=pp[:], in_=pp_d[:, :]), writes=[b_const], dsem=cs)
                S.op("sp", lambda e: e.dma_start(out=cb[:], in_=cb_d[:, :]), mwrites=[b_const], dsem=cs)
                S.op("sp", lambda e: e.dma_start(out=cf[:], in_=cf_d[:, :]), mwrites=[b_const], dsem=cs)
                S.op("sp", lambda e: e.dma_start(out=mk[:].rearrange("p a b c -> p (a b c)"), in_=mk_d[:, :]),
                     mwrites=[b_const], dsem=cs)
                ident_b = cb[:, 0:128]
                negT8 = cb[:, 256:384]
                negones8 = cb[:, 384:512]
                ident_f = cf[:, 0:128]
                ones_f = cf[:, 128:256]
                sel0 = pp[:, cfg.c_sel:cfg.c_sel + 1]
                sel1 = pp[:, cfg.c_sel + 1:cfg.c_sel + 2]

                KD = 65 if hyb else 64
                VD = 65 if hyb else 64
                nheads = 8 if hyb else 16
                voff = 512 if hyb else 1024
                half = 1024 if hyb else 2048
                xb = xb_hb if hyb else xb_sb
                KT = [Slot(S, sb(f"KT{i}", [128, S_], BF16), f"KT{i}") for i in range(2)]
                VV = [Slot(S, sb(f"V{i}", [128, NB, VD], BF16), f"V{i}") for i in range(2)]
                QT = [Slot(S, sb(f"QT{i}", [128, TL], BF16), f"QT{i}") for i in range(2)]
                psA = [pst(f"psA{i}") for i in range(2)]
                bA = [Buf() for _ in range(2)]
                psO = [pst(f"psO{i}") for i in range(2)]
                bO = [Buf() for _ in range(2)]
                Wt = [sb(f"W{i}", [128, 512], BF16) for i in range(3)]
                bW = [Buf() for _ in range(3)]
                yst = [sb(f"yst{i}", [64, 512], BF16) for i in range(2)]
                byst = [Buf() for _ in range(2)]
                ystsem = [S.dsem(f"yst{i}") for i in range(2)]
                if hyb:
                    ps_m = pst("ps_m")
                    bpm = Buf()
                    for i in range(2):
                        S.op("pool", lambda e, i=i: e.memset(KT[i].ap[64:65, :], 1.0), writes=[KT[i].buf])
                        S.op("pool", lambda e, i=i: e.memset(VV[i].ap[:, :, 64:65], 1.0), writes=[VV[i].buf])
                    cw = sb("cw", [128, S_], F32)
                    bcw = Buf()
                    cwsem = S.dsem("cw")
                    cown = sb("cown", [128, TL], F32)
                    bco = Buf()
                    cT = sb("cT", [128, NB], F32)
                    grefB = sb("grefB", [128, NG], F32)
                    bcT = Buf()
                    biasG = [sb(f"biasG{i}", [128, NB], F32) for i in range(2)]
                    bbias = [Buf() for _ in range(2)]
                    osb = sb("osb", [64, 512], F32)
                    rec = sb("rec", [128, 512], F32)
                    bos = Buf()
                else:
                    psB = [pst(f"psB{i}") for i in range(2)]
                    bB = [Buf() for _ in range(2)]
                    Et = [sb(f"E{i}", [128, 512], F32) for i in range(2)]
                    bE = [Buf() for _ in range(2)]
                    SPt = [sb(f"SP{i}", [128, 512], BF16) for i in range(3)]
                    bSP = [Buf() for _ in range(3)]
                    Rt = [sb(f"R{i}", [128, 512], BF16) for i in range(3)]
                    bR = [Buf() for _ in range(3)]

                def load_head(h):
                    k, v, q = KT[h % 2], VV[h % 2], QT[h % 2]
                    for r in range(2):
                        src = xb.rrows(r, h * 64, 64).rearrange("p (g w) -> p g w", w=512)
                        dst = k.ap[0:64, :].rearrange("p (g r w) -> p g r w", r=2, w=512)[:, :, r, :]
                        S.op("sp", lambda e, src=src, dst=dst: e.dma_start(out=dst, in_=src),
                             reads=[b_rcvb], **({"writes": [k.buf]} if r == 0 else {"mwrites": [k.buf]}), dsem=k.sem)
                        vv = xb.rrows(r, voff + h * 64, 64).rearrange("a (two c) -> (a two) c", two=2)
                        vv = vv.rearrange("p (g j d) -> p g j d", d=64, j=4)
                        dstv = v.ap[:, :, 0:64].rearrange("p (g r j) d -> p g r j d", r=2, j=4)[:, :, r, :, :]
                        if hyb:
                            for g_ in range(NG):
                                S.op("sp", lambda e, vv=vv, dstv=dstv, h=h, g_=g_: e.dma_start(out=dstv[:, g_], in_=vv[:, g_]),
                                     reads=[b_rcvb], **({"writes": [v.buf]} if (r == 0 and g_ == 0) else {"mwrites": [v.buf]}),
                                     dsem=v.sem)
                        else:
                            S.op("sp", lambda e, vv=vv, dstv=dstv, h=h: e.dma_start(out=dstv, in_=vv),
                                 reads=[b_rcvb], **({"writes": [v.buf]} if r == 0 else {"mwrites": [v.buf]}), dsem=v.sem)
                    S.op("sp", lambda e, h=h, q=q: e.dma_start(out=q.ap[0:64, :], in_=qT_d[h * 64:(h + 1) * 64, :]),
                         reads=[b_qT], writes=[q.buf], dsem=q.sem)

                if hyb:
                    rglru(l, ps, sb, pst, pp, cb, cf, b_const, sel0, sel1)

                tiles = []
                for h in range(nheads):
                    for i in range(NG):
                        lst = [(kg, j) for kg in range(2 * i + 1, -1, -1) for j in range(3, -1, -1)]
                        for n, (kg, j) in enumerate(lst):
                            mkind = None
                            if kg == 2 * i + 1:
                                mkind = 1
                            elif kg == 2 * i:
                                mkind = 0
                            tiles.append(dict(h=h, i=i, kb=kg * 4 + j, j=j, mk=mkind, first=(n == 0),
                                              last=(n == len(lst) - 1), g=h * NG + i))
                NT = len(tiles)

                def head_prep(h):
                    q = QT[h % 2]
                    for r in range(2):
                        src = rcv_lv[r, h:h + 1, :].rearrange("p (g w) -> p g w", w=512)
                        dst = cw[64:65, :].rearrange("p (g r w) -> p g r w", r=2, w=512)[:, :, r, :]
                        S.op("sp", lambda e, src=src, dst=dst: e.dma_start(out=dst, in_=src), reads=[b_rcvf],
                             **({"writes": [bcw]} if r == 0 else {"mwrites": [bcw]}), dsem=cwsem)
                    S.op("dve", lambda e: e.tensor_tensor_scan(out=cw[64:65, :], data0=cw[64:65, :], data1=cw[64:65, :],
                                                               initial=0.0, op0=ALU.add, op1=ALU.min),
                         reads=[bcw], writes=[bcw])
                    cwv = cw[64:65, :].rearrange("p (g r w) -> p g r w", r=2, w=512)
                    S.op("dve", lambda e: e.tensor_scalar(out=cown[64:65, :].rearrange("p (g w) -> p g w", w=512),
                                                          in0=cwv[:, :, 0, :], scalar1=sel0[64:65, :], scalar2=None,
                                                          op0=ALU.mult), reads=[bcw, b_const], writes=[bco])
                    S.op("dve", lambda e: e.scalar_tensor_tensor(
                        out=cown[64:65, :].rearrange("p (g w) -> p g w", w=512), in0=cwv[:, :, 1, :], scalar=sel1[64:65, :],
                        in1=cown[64:65, :].rearrange("p (g w) -> p g w", w=512), op0=ALU.mult, op1=ALU.add),
                        reads=[bcw, b_const, bco], writes=[bco])
                    for i in range(NG):
                        S.op("dve", lambda e, i=i, q=q: e.tensor_scalar(
                            out=q.ap[64:65, i * 512:(i + 1) * 512], in0=cown[64:65, i * 512:(i + 1) * 512],
                            scalar1=cown[64:65, i * 512:i * 512 + 1], scalar2=8.0, op0=ALU.subtract, op1=ALU.mult),
                            reads=[bco], mwrites=[q.buf])
                    for blk in range(NB):
                        S.op("pe", lambda e, blk=blk: e.matmul(ps_m[:, 2 * blk:2 * blk + 2], cw[64:65, blk * 128:(blk + 1) * 128],
                                                               ones_f[64:65, 0:2], start=True, stop=True),
                             reads=[bcw, b_const], writes=[bpm])
                    S.op("dve", lambda e: e.tensor_copy(out=cT[:], in_=ps_m[:, 0:2 * NB].rearrange("p (b two) -> p b two", two=2)[:, :, 0]),
                         reads=[bpm], writes=[bcT])
                    S.op("pe", lambda e: e.matmul(ps_m[:, 256:256 + NG], ones_f[64:65, :],
                                                  cown[64:65, :].rearrange("p (g w) -> p g w", w=512)[:, :, 0],
                                                  start=True, stop=True), reads=[bco, b_const, bcT], writes=[bpm])
                    S.op("dve", lambda e: e.tensor_copy(out=grefB[:], in_=ps_m[:, 256:256 + NG]), reads=[bpm], writes=[bcT])

                def group_bias(g, i):
                    S.op("dve", lambda e: e.tensor_scalar(out=biasG[g % 2][:], in0=cT[:], scalar1=-1.0,
                                                          scalar2=grefB[:, i:i + 1], op0=ALU.mult, op1=ALU.add),
                         reads=[bcT], writes=[bbias[g % 2]])

                def emitA(s):
                    t = tiles[s]
                    h, i, kb = t["h"], t["i"], t["kb"]
                    k, q = KT[h % 2], QT[h % 2]
                    pa, ba = psA[s % 2], bA[s % 2]
                    S.op("pe", lambda e: e.matmul(pa[:], k.ap[0:KD, kb * 128:(kb + 1) * 128], q.ap[0:KD, i * 512:(i + 1) * 512],
                                                  start=True, stop=(t["mk"] is None)), reads=[k.buf, q.buf], writes=[ba])
                    if t["mk"] is not None:
                        mi = (0 if hyb else 2) + t["mk"]
                        S.op("pe", lambda e: e.matmul(pa[:], ident_b, mk[:, mi, t["j"], :], start=False, stop=True),
                             reads=[b_const], writes=[ba])

                def evac_group(t):
                    g, h, i = t["g"], t["h"], t["i"]
                    po, bo = psO[g % 2], bO[g % 2]
                    ys, bys, yss = yst[g % 2], byst[g % 2], ystsem[g % 2]
                    if hyb:
                        S.op("dve", lambda e: e.tensor_copy(out=osb[:], in_=po[0:64, :]), reads=[bo], writes=[bos])
                        S.op("dve", lambda e: e.reciprocal(out=rec[64:65, :], in_=po[64:65, :]), reads=[bo], writes=[bos])
                        S.op("pe", lambda e: e.matmul(ps_m[0:64, :], ones_f[64:65, 0:64], rec[64:65, :], start=True, stop=True),
                             reads=[bos, b_const, bcT], writes=[bpm])
                        S.op("dve", lambda e: e.tensor_tensor(out=ys[:], in0=osb[:], in1=ps_m[0:64, :], op=ALU.mult),
                             reads=[bos, bpm], writes=[bys])
                        row = 512 + h * 64
                    else:
                        S.op("dve", lambda e: e.tensor_copy(out=ys[:], in_=po[0:64, :]), reads=[bo], writes=[bys])
                        row = h * 64
                    S.op("sp", lambda e: e.dma_start(out=y_d[row:row + 64, i * 512:(i + 1) * 512], in_=ys[:]),
                         reads=[bys], mwrites=[b_y[i]], dsem=yss)

                load_head(0)
                if hyb:
                    head_prep(0)
                emitA(0)
                for s in range(NT):
                    t = tiles[s]
                    h, i, kb, g = t["h"], t["i"], t["kb"], t["g"]
                    k, v, q = KT[h % 2], VV[h % 2], QT[h % 2]
                    if t["first"] and i == 0 and h + 1 < nheads:
                        load_head(h + 1)
                    if hyb and t["first"]:
                        group_bias(g, i)
                    if s + 1 < NT:
                        if hyb and tiles[s + 1]["h"] != h:
                            head_prep(h + 1)
                        emitA(s + 1)
                    pa, ba = psA[s % 2], bA[s % 2]
                    if hyb:
                        w, bw = Wt[s % 3], bW[s % 3]
                        S.op("act", lambda e, pa=pa, w=w, g=g, kb=kb: e.activation(
                            out=w[:], in_=pa[:], func=AF.Exp, bias=biasG[g % 2][:, kb:kb + 1], scale=0.125),
                            reads=[ba, bbias[g % 2]], writes=[bw])
                        po, bo = psO[g % 2], bO[g % 2]
                        S.op("pe", lambda e, po=po, v=v, w=w, kb=kb, t=t: e.matmul(
                            po[0:65, :], v.ap[:, kb, :], w[:], start=t["first"], stop=t["last"]),
                            reads=[v.buf, bw], writes=[bo])
                        if t["last"]:
                            evac_group(t)
                    else:
                        et, bet = Et[s % 2], bE[s % 2]
                        sp_, bsp = SPt[s % 3], bSP[s % 3]
                        S.op("act", lambda e, pa=pa, et=et: e.activation(out=et[:], in_=pa[:], func=AF.Exp, scale=0.125),
                             reads=[ba], writes=[bet])
                        S.op("act", lambda e, et=et, sp_=sp_: e.activation(out=sp_[:], in_=et[:], func=AF.Ln, bias=1.0),
                             reads=[bet], writes=[bsp])
                        rn, brn = Rt[s % 3], bR[s % 3]
                        ro, bro = Rt[(s - 1) % 3], bR[(s - 1) % 3]
                        if t["first"]:
                            S.op("pool", lambda e, rn=rn, sp_=sp_: e.tensor_copy(out=rn[:], in_=sp_[:]), reads=[bsp], writes=[brn])
                        else:
                            S.op("pool", lambda e, rn=rn, ro=ro, sp_=sp_: e.tensor_tensor(out=rn[:], in0=ro[:], in1=sp_[:], op=ALU.add),
                                 reads=[bsp, bro], writes=[brn])
                        pb, bb = psB[s % 2], bB[s % 2]
                        S.op("pe", lambda e, pb=pb, k=k, q=q, kb=kb, i=i: e.matmul(
                            pb[:], k.ap[0:64, kb * 128:(kb + 1) * 128], q.ap[0:64, i * 512:(i + 1) * 512], start=True, stop=False),
                            reads=[k.buf, q.buf], writes=[bb])
                        if t["mk"] is not None:
                            S.op("pe", lambda e, pb=pb, t=t: e.matmul(pb[:], ident_b, mk[:, 2 + t["mk"], t["j"], :], start=False, stop=False),
                                 reads=[b_const], writes=[bb])
                        S.op("pe", lambda e, pb=pb, sp_=sp_, t=t: e.matmul(pb[:], negT8, sp_[:], start=False, stop=t["first"]),
                             reads=[bsp, b_const], writes=[bb])
                        if not t["first"]:
                            S.op("pe", lambda e, pb=pb, ro=ro: e.matmul(pb[:], negones8, ro[:], start=False, stop=True),
                                 reads=[bro, b_const], writes=[bb])
                        if s >= 1:
                            sb_pv(s - 1, tiles, psB, bB, Wt, bW, psO, bO, VV, evac_group)
                if not hyb:
                    sb_pv(NT - 1, tiles, psB, bB, Wt, bW, psO, bO, VV, evac_group)
                S.emit()

        def sb_pv(s, tiles, psB, bB, Wt, bW, psO, bO, VV, evac_group):
            t = tiles[s]
            g, h, kb = t["g"], t["h"], t["kb"]
            pb, bb = psB[s % 2], bB[s % 2]
            w, bw = Wt[s % 3], bW[s % 3]
            v = VV[h % 2]
            S.op("act", lambda e: e.activation(out=w[:], in_=pb[:], func=AF.Exp, scale=0.125), reads=[bb], writes=[bw])
            po, bo = psO[g % 2], bO[g % 2]
            S.op("pe", lambda e: e.matmul(po[0:64, :], v.ap[:, kb, :], w[:], start=t["first"], stop=t["last"]),
                 reads=[v.buf, bw], writes=[bo])
            if t["last"]:
                evac_group(t)

        def rglru(l, ps, sb, pst, pp, cb, cf, b_const, sel0, sel1):
            e_ = l // 2
            SEG = min(1024, S_)
            NSEG = S_ // SEG
            OWN = SEG // 2
            gw = sb("gw", [128, 2, 4, 128], BF16)
            bgw = Buf()
            S.op("sp", lambda e: e.dma_start(out=gw[:].rearrange("p a b c -> p (a b c)"), in_=gw_b[e_]),
                 reads=[b_gw[e_]], writes=[bgw], dsem=S.dsem("gw"))
            rgd = sb("rgd", [128, 4, 4], F32)
            brgd = Buf()
            uS = [sb(f"uS{i}", [128, SEG + 3], F32) for i in range(2)]
            buS = [Buf() for _ in range(2)]
            usem = [S.dsem(f"uS{i}") for i in range(2)]
            xc = sb("xc", [128, SEG], F32)
            xcb = sb("xcb", [128, SEG], BF16)
            bxc = Buf()
            r_t = sb("r_t", [128, SEG], F32)
            i_t = sb("i_t", [128, SEG], F32)
            a_t = sb("a_t", [128, SEG], F32)
            s_t = sb("s_t", [128, SEG], F32)
            bri, ba_, bs_ = Buf(), Buf(), Buf()
            h_t = [sb(f"h_t{i}", [128, SEG], F32) for i in range(2)]
            bht = [Buf() for _ in range(2)]
            gt = sb("gt", [128, OWN], F32)
            bgt = Buf()
            gsem = S.dsem("gt")
            ho = sb("ho", [128, OWN], F32)
            bho = Buf()
            yo = sb("yo", [128, OWN], BF16)
            byo = Buf()
            yosem = S.dsem("yo")
            psg = [pst(f"psg{i}") for i in range(2)]
            bpg = [Buf() for _ in range(2)]
            npg = [0]
            nseg = 0
            for c in range(4):
                col = cfg.c_rg + (e_ * 4 + c) * 8
                S.op("act", lambda e, c=c, col=col: e.activation(out=rgd[:, c, 0:1], in_=pp[:, col + 7:col + 8], func=AF.Exp, scale=-1.0),
                     reads=[b_const], writes=[brgd])
                S.op("act", lambda e, c=c: e.activation(out=rgd[:, c, 1:2], in_=rgd[:, c, 0:1], func=AF.Ln, bias=1.0),
                     reads=[brgd], writes=[brgd])
                S.op("dve", lambda e, c=c: e.tensor_scalar(out=rgd[:, c, 2:3], in0=rgd[:, c, 1:2], scalar1=-8.0, scalar2=None, op0=ALU.mult),
                     reads=[brgd], writes=[brgd])
                S.op("dve", lambda e, c=c: e.tensor_scalar(out=rgd[:, c, 3:4], in0=rgd[:, c, 1:2], scalar1=-16.0, scalar2=None, op0=ALU.mult),
                     reads=[brgd], writes=[brgd])
                for sg in range(NSEG):
                    u, bu, us = uS[nseg % 2], buS[nseg % 2], usem[nseg % 2]
                    up, bup = uS[(nseg - 1) % 2], buS[(nseg - 1) % 2]
                    hcur, bhc = h_t[nseg % 2], bht[nseg % 2]
                    hprev, bhp = h_t[(nseg - 1) % 2], bht[(nseg - 1) % 2]
                    nseg += 1
                    g0 = sg * (SEG // 512)
                    for r in range(2):
                        src = xb_f.rrows(r, c * 128, 128)[:, (g0 // 2) * 512:(g0 // 2) * 512 + OWN]
                        src = src.rearrange("p (g w) -> p g w", w=512)
                        dst = u[:, 3:3 + SEG].rearrange("p (g r w) -> p g r w", r=2, w=512)[:, :, r, :]
                        S.op("sp", lambda e, src=src, dst=dst: e.dma_start(out=dst, in_=src), reads=[b_rcvf],
                             **({"writes": [bu]} if r == 0 else {"mwrites": [bu]}), dsem=us)
                    S.op("sp", lambda e, c=c, sg=sg: e.dma_start(out=gt[:], in_=gate_d[c * 128:(c + 1) * 128, sg * OWN:(sg + 1) * OWN]),
                         reads=[b_gate], writes=[bgt], dsem=gsem)
                    if sg == 0:
                        S.op("pool", lambda e, u=u: e.memset(u[:, 0:3], 0.0), mwrites=[bu])
                    else:
                        S.op("pool", lambda e, u=u, up=up: e.tensor_copy(out=u[:, 0:3], in_=up[:, SEG:SEG + 3]),
                             reads=[bup], mwrites=[bu])
                    S.op("dve", lambda e, u=u, col=col: e.tensor_scalar(
                        out=xc[:], in0=u[:, 3:3 + SEG], scalar1=pp[:, col + 3:col + 4], scalar2=pp[:, col + 4:col + 5],
                        op0=ALU.mult, op1=ALU.add), reads=[bu, b_const], writes=[bxc])
                    for j in range(3):
                        S.op("dve", lambda e, u=u, col=col, j=j: e.scalar_tensor_tensor(
                            out=xc[:], in0=u[:, j:j + SEG], scalar=pp[:, col + j:col + j + 1], in1=xc[:],
                            op0=ALU.mult, op1=ALU.add), reads=[bu, b_const, bxc], writes=[bxc])
                    S.op("act", lambda e: e.activation(out=xcb[:], in_=xc[:], func=AF.Copy), reads=[bxc], writes=[bxc])
                    for qq in range(SEG // 512):
                        for gi, dst_t in ((0, r_t), (1, i_t)):
                            pg, bg_ = psg[npg[0] % 2], bpg[npg[0] % 2]
                            npg[0] += 1
                            S.op("pe", lambda e, pg=pg, gi=gi, c=c, qq=qq: e.matmul(
                                pg[:], gw[:, gi, c, :], xcb[:, qq * 512:(qq + 1) * 512], start=True, stop=True),
                                reads=[bgw, bxc], writes=[bg_])
                            S.op("act", lambda e, pg=pg, gi=gi, dst_t=dst_t, qq=qq, col=col: e.activation(
                                out=dst_t[:, qq * 512:(qq + 1) * 512], in_=pg[:], func=AF.Sigmoid,
                                bias=pp[:, col + 5 + gi:col + 6 + gi]), reads=[bg_, b_const], mwrites=[bri])
                    S.op("act", lambda e, c=c: e.activation(out=a_t[:], in_=r_t[:], func=AF.Exp, scale=rgd[:, c, 2:3]),
                         reads=[bri, brgd], writes=[ba_])
                    S.op("act", lambda e, c=c: e.activation(out=s_t[:], in_=r_t[:], func=AF.Exp, scale=rgd[:, c, 3:4]),
                         reads=[bri, brgd], writes=[bs_])
                    S.op("act", lambda e: e.activation(out=s_t[:], in_=s_t[:], func=AF.Sqrt, bias=1.0, scale=-1.0),
                         reads=[bs_], writes=[bs_])
                    S.op("dve", lambda e: e.tensor_tensor(out=i_t[:], in0=i_t[:], in1=xc[:], op=ALU.mult),
                         reads=[bri, bxc], writes=[bri])
                    S.op("dve", lambda e: e.tensor_tensor(out=s_t[:], in0=s_t[:], in1=i_t[:], op=ALU.mult),
                         reads=[bri, bs_], writes=[bs_])
                    init = 0.0 if sg == 0 else hprev[:, SEG - 1:SEG]
                    S.op("dve", lambda e, hcur=hcur, init=init: e.tensor_tensor_scan(
                        out=hcur[:], data0=a_t[:], data1=s_t[:], initial=init, op0=ALU.mult, op1=ALU.add),
                        reads=[ba_, bs_] + ([bhp] if sg > 0 else []), writes=[bhc])
                    hv = hcur[:].rearrange("p (g r w) -> p g r w", r=2, w=512)
                    S.op("dve", lambda e, hv=hv: e.tensor_scalar(out=ho[:].rearrange("p (g w) -> p g w", w=512), in0=hv[:, :, 0, :],
                                                                 scalar1=sel0, scalar2=None, op0=ALU.mult),
                         reads=[bhc, b_const], writes=[bho])
                    S.op("dve", lambda e, hv=hv: e.scalar_tensor_tensor(
                        out=ho[:].rearrange("p (g w) -> p g w", w=512), in0=hv[:, :, 1, :], scalar=sel1,
                        in1=ho[:].rearrange("p (g w) -> p g w", w=512), op0=ALU.mult, op1=ALU.add),
                        reads=[bhc, b_const, bho], writes=[bho])
                    S.op("act", lambda e: e.activation(out=gt[:], in_=gt[:], func=AF.Gelu_apprx_tanh), reads=[bgt], writes=[bgt])
                    S.op("dve", lambda e: e.tensor_tensor(out=yo[:], in0=ho[:], in1=gt[:], op=ALU.mult),
                         reads=[bho, bgt], writes=[byo])
                    for gg in range(OWN // 512):
                        i = sg * (OWN // 512) + gg
                        S.op("sp", lambda e, c=c, i=i, gg=gg: e.dma_start(
                            out=y_d[c * 128:(c + 1) * 128, i * 512:(i + 1) * 512], in_=yo[:, gg * 512:(gg + 1) * 512]),
                            reads=[byo], mwrites=[b_y[i]], dsem=yosem)

        cvt_ffn(0, 0)
        cvt_mix(0)
        cvt_ffn(0, 1)
        if depth > 1:
            cvt_ffn(1, 0)
        import os
        stop = int(os.environ.get("K_STOP", "99"))
        token_pass("first", 0)
        for l in range(depth):
            if stop <= 3 * l:
                break
            exchange(l % 2 == 0)
            if stop <= 3 * l + 1:
                break
            if l + 1 < depth:
                cvt_mix(l + 1)
                cvt_ffn(l + 1, 1)
                if l + 2 < depth:
                    cvt_ffn(l + 2, 0)
            attn_phase(l)
            if stop <= 3 * l + 2:
                break
            token_pass("mid" if l + 1 < depth else "last", l)
        S.finish()
    return nc


def _arrange(cfg, inp):
    depth = cfg.depth
    f32 = np.float32
    w_in = np.asarray(inp["ffn_w_in"], f32)[:depth]
    a = np.zeros((depth, 2, 6, 128, 8, 2, 512), f32)
    wi = w_in.reshape(depth, 2, 8, 128, 2, DFF)
    for s in range(6):
        n = 512 if s < 5 else 256
        a[:, :, s, :, :, :, :n] = wi[:, :, :, :, :, s * 512:s * 512 + n].transpose(0, 1, 3, 2, 4, 5)
    w_in_r = a.reshape(depth, 2, 6, 128, 8 * 2 * 512)
    w_out = np.asarray(inp["ffn_w_out"], f32)[:depth]
    w_out_r = np.ascontiguousarray(w_out.reshape(depth, 2, NCH, 128, 8, 128).transpose(0, 1, 4, 3, 2, 5)).reshape(
        depth, 2, 8, 128, NCH * 128)
    w_mi = np.zeros((depth, 3, 128, 8, 2, 512), f32)
    w_mo = np.zeros((depth, 128, 8, 8, 128), f32)
    for l in range(depth):
        if l % 2 == 0:
            w = np.asarray(inp["hy_w_in"][l // 2], f32).reshape(8, 128, 2568)
            for s in range(2):
                for ab in range(2):
                    c0 = (s * 2 + ab) * 512
                    w_mi[l, s, :, :, ab, :] = w[:, :, c0:c0 + 512].transpose(1, 0, 2)
            w_mi[l, 2, :, :, 0, :] = w[:, :, 2048:2560].transpose(1, 0, 2)
            w_mi[l, 2, :, :, 1, 0:8] = w[:, :, 2560:2568].transpose(1, 0, 2)
            wo = np.asarray(inp["hy_w_out"][l // 2], f32)
        else:
            w = np.asarray(inp["sb_w_qkv"][l // 2], f32).reshape(8, 128, 3072)
            for s in range(3):
                for ab in range(2):
                    c0 = (s * 2 + ab) * 512
                    w_mi[l, s, :, :, ab, :] = w[:, :, c0:c0 + 512].transpose(1, 0, 2)
            wo = np.asarray(inp["sb_w_out"][l // 2], f32)
        w_mo[l] = wo.reshape(8, 128, 8, 128).transpose(1, 2, 0, 3)
    ne = max(1, cfg.n_even)
    gw = np.zeros((ne, 128, 2, 4, 128), f32)
    pp = np.zeros((128, cfg.NPP), f32)
    p = np.arange(128)
    for l in range(depth):
        for i in range(2):
            pp[:, cfg.c_ffn + (l * 2 + i) * 8:cfg.c_ffn + (l * 2 + i) * 8 + 8] = np.asarray(inp["ffn_norm"][l, i], f32).reshape(8, 128).T
        pp[:, cfg.c_mix + l * 8:cfg.c_mix + l * 8 + 8] = np.asarray(inp["mix_norm"][l], f32).reshape(8, 128).T
    for e in range(cfg.n_even):
        g = np.asarray(inp["rg_gate_w"][e], f32)
        for gi in range(2):
            for c in range(4):
                for hh in range(2):
                    gw[e, hh * 64:(hh + 1) * 64, gi, c, hh * 64:(hh + 1) * 64] = g[gi, c * 2 + hh]
        qk = np.asarray(inp["fox_qk_norm"][e], f32)
        pp[:, cfg.c_qk + e * 2 + 0] = qk[0][p % 64]
        pp[:, cfg.c_qk + e * 2 + 1] = qk[1][p % 64]
        for c in range(4):
            col = cfg.c_rg + (e * 4 + c) * 8
            for j in range(4):
                pp[:, col + j] = np.asarray(inp["rg_conv_w"][e, j], f32)[c * 128:(c + 1) * 128]
            pp[:, col + 4] = np.asarray(inp["rg_conv_b"][e], f32)[c * 128:(c + 1) * 128]
            pp[:, col + 5] = np.asarray(inp["rg_gate_b"][e, 0], f32)[c * 128:(c + 1) * 128]
            pp[:, col + 6] = np.asarray(inp["rg_gate_b"][e, 1], f32)[c * 128:(c + 1) * 128]
            pp[:, col + 7] = np.asarray(inp["rg_lambda"][e], f32)[c * 128:(c + 1) * 128]
        pp[0:8, cfg.c_fb + e] = np.asarray(inp["fox_fgate_b"][e], f32)
    return dict(w_in=w_in_r, w_out=w_out_r, w_mi=w_mi.reshape(depth, 3, 128, 8 * 2 * 512),
                w_mo=w_mo.reshape(depth, 128, 8 * 8 * 128), gate_w=gw.reshape(ne, 128, 2 * 4 * 128)), pp


def _consts(rank):
    bf = ml_dtypes.bfloat16
    cb = np.zeros((128, 640), np.float32)
    j = np.arange(128)[:, None]
    k = np.arange(128)[None, :]
    cb[:, 0:128] = np.eye(128)
    cb[:, 128:256] = 1.0
    cb[:, 256:384] = np.where(j >= k, -8.0, 0.0)
    cb[:, 384:512] = -8.0
    cb[:, 512:640] = (j // 64 == k // 64)
    mk = np.zeros((4, 4, 128, 512), np.float32)
    kk = np.arange(128)[:, None]
    q = np.arange(512)[None, :]
    for jb in range(4):
        kpos = jb * 128 + kk
        diag_fox = np.where(kpos > q, MASKV, 0.0)
        diag_sb = np.where(kpos >= q, MASKV, 0.0)
        full = np.full((128, 512), MASKV)
        zero = np.zeros((128, 512))
        if rank == 0:
            mk[0, jb], mk[1, jb], mk[2, jb], mk[3, jb] = diag_fox, full, diag_sb, full
        else:
            mk[0, jb], mk[1, jb], mk[2, jb], mk[3, jb] = zero, diag_fox, zero, diag_sb
    mk = np.ascontiguousarray(mk.transpose(2, 0, 1, 3)).reshape(128, 4 * 4 * 512)
    cf = np.zeros((128, 256), np.float32)
    cf[:, 0:128] = np.eye(128)
    cf[:, 128:256] = 1.0
    return cb.astype(bf), mk.astype(bf), cf


_PROG_CACHE = {}


def run(cfg, inp):
    x = np.asarray(inp["x"], np.float32)
    B = x.shape[0]
    assert B == cfg.npairs
    key = (cfg.S, cfg.depth, cfg.npairs)
    if key not in _PROG_CACHE:
        _PROG_CACHE[key] = build_program(cfg)
    nc = _PROG_CACHE[key]
    wts, pp = _arrange(cfg, inp)
    in_maps = []
    for core in range(2 * B):
        b, r = core // 2, core % 2
        xs = np.ascontiguousarray(x[b].reshape(cfg.NGG, 512, D)[r::2].reshape(cfg.TL, D))
        cb, mk, cf = _consts(r)
        ppc = pp.copy()
        ppc[:, cfg.c_sel] = 1.0 - r
        ppc[:, cfg.c_sel + 1] = float(r)
        m = dict(wts)
        m.update(x=xs, pp=ppc, cb=cb, mk=mk, cf=cf)
        in_maps.append(m)
    res = run_bass_kernel_spmd(nc, in_maps, core_ids=list(range(2 * B)))
    out = np.zeros((B, cfg.S, D), np.float32)
    for core in range(2 * B):
        b, r = core // 2, core % 2
        o = np.asarray(res.results[core]["out"], np.float32).reshape(cfg.NG, 512, D)
        out[b].reshape(cfg.NGG, 512, D)[r::2] = o
    return out


def kernel(**inputs):
    cfg = Cfg(S=8192, depth=4, npairs=4)
    return run(cfg, inputs)
```

```python
import numpy as np
import ml_dtypes
from contextlib import ExitStack
import concourse.bass as bass
import concourse.mybir as mybir
from concourse.bass_utils import run_bass_kernel_spmd

F32 = mybir.dt.float32
BF16 = mybir.dt.bfloat16
AF = mybir.ActivationFunctionType
ALU = mybir.AluOpType

D = 1024
DFF = 2816
NCH = 22
RMS_EPS = 1e-6
MASKV = -30000.0
ENGS = ["pe", "act", "dve", "pool", "sp"]
BLK = {"pe": "tensor", "act": "scalar", "dve": "vector", "pool": "gpsimd", "sp": "sync"}


class Cfg:
    def __init__(self, S=8192, depth=4, npairs=4, cc_bytes=2 * 1024 * 1024):
        self.S = S
        self.TL = S // 2
        self.NG = self.TL // 512
        self.NGG = S // 512
        self.NB = S // 128
        self.depth = depth
        self.npairs = npairs
        self.n_even = (depth + 1) // 2
        self.n_odd = depth // 2
        self.cc_bytes = cc_bytes
        c = 0
        self.c_ffn = c; c += depth * 2 * 8
        self.c_mix = c; c += depth * 8
        self.c_qk = c; c += max(1, self.n_even) * 2
        self.c_rg = c; c += max(1, self.n_even) * 4 * 8
        self.c_fb = c; c += max(1, self.n_even)
        self.c_sel = c; c += 2
        self.NPP = c


class Buf:
    __slots__ = ("w", "r", "mw")

    def __init__(self):
        self.w = None
        self.r = {}
        self.mw = {}


class DSem:
    def __init__(self, key):
        self.key = key
        self.val = 0


class Sched:
    EPOCH = 28000

    def __init__(self, nc, stack):
        self.nc = nc
        self.stack = stack
        self.sems = []
        self.ops = {e: [] for e in ENGS}
        self.cnt = {e: 0 for e in ENGS}
        self.nep = {e: 0 for e in ENGS}
        self.cur = {e: self._new(f"{e}_0") for e in ENGS}
        self.seen = {e: {} for e in ENGS}
        self.pend = {e: [] for e in ENGS}
        self.dsems = []
        self.dsem_names = {}
        self.final = {}

    def _new(self, name):
        s = self.stack.enter_context(self.nc.semaphore(name))
        self.sems.append(s)
        return len(self.sems) - 1

    def dsem(self, name):
        if name in self.dsem_names:
            return self.dsem_names[name]
        d = DSem(self._new(name))
        self.dsems.append(d)
        self.dsem_names[name] = d
        return d

    def op(self, e, fn, reads=(), writes=(), mwrites=(), dsem=None, inc=16):
        waits = {}

        def need(k, v):
            if waits.get(k, 0) < v:
                waits[k] = v

        for b in reads:
            if b.w is not None:
                need(*b.w)
            for k, v in b.mw.items():
                need(k, v)
        for b in writes:
            if b.w is not None:
                need(*b.w)
            for k, v in b.r.items():
                need(k, v)
            for k, v in b.mw.items():
                need(k, v)
        for b in mwrites:
            if b.w is not None:
                need(*b.w)
            for k, v in b.r.items():
                need(k, v)
        for k, v in self.pend[e]:
            need(k, v)
        self.pend[e] = []
        seen = self.seen[e]
        wl = []
        own = self.cur[e]
        for k, v in waits.items():
            if seen.get(k, 0) >= v:
                continue
            seen[k] = v
            if e == "pe" and k == own:
                continue
            wl.append((k, v))
        if dsem is None:
            if self.cnt[e] >= self.EPOCH:
                self.nep[e] += 1
                self.cur[e] = self._new(f"{e}_{self.nep[e]}")
                self.cnt[e] = 0
            self.cnt[e] += 1
            h = (self.cur[e], self.cnt[e])
            incv = 1
        else:
            dsem.val += inc
            h = (dsem.key, dsem.val)
            incv = inc
        for b in writes:
            b.w = h
            b.r = {}
            b.mw = {}
        for b in mwrites:
            if b.mw.get(h[0], 0) < h[1]:
                b.mw[h[0]] = h[1]
        for b in reads:
            if b.r.get(h[0], 0) < h[1]:
                b.r[h[0]] = h[1]
        self.ops[e].append((wl, fn, (h[0], incv)))
        self.final[h[0]] = max(self.final.get(h[0], 0), h[1])
        return h

    def barrier(self):
        hs = [(self.cur[e], self.cnt[e]) for e in ENGS if self.cnt[e] > 0]
        hs += [(d.key, d.val) for d in self.dsems if d.val > 0]
        for e in ENGS:
            self.pend[e] = list(hs)

    def emit(self):
        with self.nc.Block() as block:
            for e in ENGS:
                ops = self.ops[e]
                if not ops:
                    continue

                def body(engine, ops=ops):
                    for wl, fn, (k, incv) in ops:
                        for wk, wv in wl:
                            engine.wait_ge(self.sems[wk], wv)
                        ins = fn(engine)
                        ins.then_inc(self.sems[k], incv)

                getattr(block, BLK[e])(body)
        self.ops = {e: [] for e in ENGS}

    def finish(self):
        with self.nc.Block() as block:
            fin = dict(self.final)

            def body(engine):
                for k, v in fin.items():
                    engine.wait_ge(self.sems[k], v)

            block.sync(body)


class Slot:
    def __init__(self, S, ap, name):
        self.ap = ap
        self.buf = Buf()
        self.sem = S.dsem(name)


def build_program(cfg):
    nc = bass.Bass("TRN2", target_bir_lowering=False)
    S_, TL, NG, NGG, NB, depth = cfg.S, cfg.TL, cfg.NG, cfg.NGG, cfg.NB, cfg.depth
    ne, no = max(1, cfg.n_even), max(1, cfg.n_odd)

    def din(name, shape, dt):
        return nc.dram_tensor(name, list(shape), dt, kind="ExternalInput")

    x_in = din("x", [TL, D], F32)
    w_in_f = din("w_in", [depth, 2, 6, 128, 8 * 2 * 512], F32)
    w_out_f = din("w_out", [depth, 2, 8, 128, NCH * 128], F32)
    w_mi_f = din("w_mi", [depth, 3, 128, 8 * 2 * 512], F32)
    w_mo_f = din("w_mo", [depth, 128, 8 * 8 * 128], F32)
    gw_f = din("gate_w", [ne, 128, 2 * 4 * 128], F32)
    pp_d = din("pp", [128, cfg.NPP], F32)
    cb_d = din("cb", [128, 640], BF16)
    mk_d = din("mk", [128, 4 * 4 * 512], BF16)
    cf_d = din("cf", [128, 256], F32)
    out_d = nc.dram_tensor("out", [TL, D], F32, kind="ExternalOutput")

    w_in_b = nc.dram_tensor("w_in_b", [depth, 2, 6, 128, 8 * 2 * 512], BF16)
    w_out_b = nc.dram_tensor("w_out_b", [depth, 2, 8, 128, NCH * 128], BF16)
    w_mi_b = nc.dram_tensor("w_mi_b", [depth, 3, 128, 8 * 2 * 512], BF16)
    w_mo_b = nc.dram_tensor("w_mo_b", [depth, 128, 8 * 8 * 128], BF16)
    gw_b = nc.dram_tensor("gw_b", [ne, 128, 2 * 4 * 128], BF16)
    xT_d = nc.dram_tensor("xT", [8, 128, TL], F32)
    qT_d = nc.dram_tensor("qT", [1024, TL], BF16)
    y_d = nc.dram_tensor("yT", [1024, TL], BF16)
    gate_d = nc.dram_tensor("gateT", [512, TL], F32)
    CCB = cfg.cc_bytes

    class XB:
        def __init__(self, name, rows, dt, esz, cols=None):
            cols = cols or TL
            self.CR = min(rows, max(1, CCB // (cols * esz)))
            assert rows % self.CR == 0
            self.n = rows // self.CR
            self.snd = [nc.dram_tensor(f"{name}_s{c}", [self.CR, cols], dt) for c in range(self.n)]
            self.rcv = [nc.dram_tensor(f"{name}_r{c}", [2 * self.CR, cols], dt) for c in range(self.n)]
            self.rbuf = [Buf() for _ in range(self.n)]

        def rb(self, r0):
            return self.rbuf[r0 // self.CR]

        def srows(self, r0, n):
            c, o = divmod(r0, self.CR)
            assert o + n <= self.CR, (r0, n, self.CR)
            return self.snd[c][o:o + n, :]

        def rrows(self, rank, r0, n):
            c, o = divmod(r0, self.CR)
            assert o + n <= self.CR
            return self.rcv[c][rank * self.CR + o:rank * self.CR + o + n, :]

    xb_sb = XB("xsb", 2048, BF16, 2)
    xb_hb = XB("xhb", 1024, BF16, 2)
    xb_f = XB("xf", 512, F32, 4)
    snd_l = nc.dram_tensor("snd_l", [512, TL // 64], F32)
    rcv_l = nc.dram_tensor("rcv_l", [1024, TL // 64], F32)
    snd_lv = snd_l.ap().rearrange("(h a) b -> h (a b)", h=8)
    rcv_lv = rcv_l.ap().rearrange("(r h a) b -> r h (a b)", r=2, h=8)
    RG = [[2 * i, 2 * i + 1] for i in range(cfg.npairs)]

    gstack = ExitStack()
    with gstack:
        S = Sched(nc, gstack)
        b_win = [[[Buf() for _ in range(6)] for _ in range(2)] for _ in range(depth)]
        b_wout = [[[Buf() for _ in range(8)] for _ in range(2)] for _ in range(depth)]
        b_wmi = [[Buf() for _ in range(3)] for _ in range(depth)]
        b_wmo = [Buf() for _ in range(depth)]
        b_gw = [Buf() for _ in range(ne)]
        b_xT = [Buf() for _ in range(NG)]
        b_y = [Buf() for _ in range(NG)]
        b_qT = Buf()
        b_gate = Buf()
        b_sndb, b_rcvb, b_sndf, b_rcvf = Buf(), Buf(), Buf(), Buf()
        b_out = Buf()
        cvt_sems = [S.dsem(f"cvt{i}") for i in range(8)]
        cvt_n = [0]
        cc_sem = S.dsem("cc")
        uid = [0]

        def cvt(dst, src, buf):
            ds_ = cvt_sems[cvt_n[0] % len(cvt_sems)]
            cvt_n[0] += 1
            n = 1
            for s_ in src.shape:
                n *= s_
            assert n % 2048 == 0 or n < 2048 * 16, n
            S.op("pool", lambda e, dst=dst, src=src: e.dma_start(out=dst, in_=src), writes=[buf], dsem=ds_)

        def flat2(ap, cols=2048):
            sh = ap.shape
            if len(sh) == 2:
                return ap.rearrange("a (b c) -> (a b) c", c=cols)
            return ap.rearrange("m a (b c) -> (m a b) c", c=cols)

        def cvt_ffn(l, i):
            for s_ in range(6):
                cvt(flat2(w_in_b[l, i, s_]), flat2(w_in_f[l, i, s_]), b_win[l][i][s_])
            for m in range(8):
                cvt(flat2(w_out_b[l, i, m], 1408), flat2(w_out_f[l, i, m], 1408), b_wout[l][i][m])

        def cvt_mix(l):
            for s_ in range(3):
                cvt(flat2(w_mi_b[l, s_]), flat2(w_mi_f[l, s_]), b_wmi[l][s_])
            cvt(flat2(w_mo_b[l]), flat2(w_mo_f[l]), b_wmo[l])
            if l % 2 == 0:
                cvt(gw_b[l // 2], gw_f[l // 2], b_gw[l // 2])

        def token_pass(kind, l):
            with ExitStack() as ps:
                uid[0] += 1
                u_ = uid[0]

                def sb(name, shape, dt, u_=u_):
                    return ps.enter_context(nc.sbuf_tensor(f"s{u_}_{name}", list(shape), dt))

                def pst(name, shape=(128, 512), dt=F32, u_=u_):
                    return ps.enter_context(nc.psum_tensor(f"p{u_}_{name}", list(shape), dt))

                S.barrier()
                pp = sb("pp", [128, cfg.NPP], F32)
                cb = sb("cb", [128, 640], BF16)
                cf = sb("cf", [128, 256], F32)
                ident_f = cf[:, 0:128]
                ones_b = cb[:, 128:256]
                bd_b = cb[:, 512:640]
                b_const = Buf()
                cs = S.dsem("tp_const")
                S.op("sp", lambda e: e.dma_start(out=pp[:], in_=pp_d[:, :]), writes=[b_const], dsem=cs)
                S.op("sp", lambda e: e.dma_start(out=cb[:], in_=cb_d[:, :]), mwrites=[b_const], dsem=cs)
                S.op("sp", lambda e: e.dma_start(out=cf[:], in_=cf_d[:, :]), mwrites=[b_const], dsem=cs)
                nfb = sb("nfb", [128, ne], F32)
                S.op("dve", lambda e: e.tensor_scalar(out=nfb[:], in0=pp[:, cfg.c_fb:cfg.c_fb + ne], scalar1=-1.0,
                                                      scalar2=None, op0=ALU.mult), reads=[b_const], writes=[b_const])

                xT = sb("xT", [128, 8, 512], F32)
                bx = Buf()
                xsem = S.dsem("xT")
                hT = sb("hT", [128, 8, 512], BF16)
                bh = Buf()
                gT = sb("gT", [128, NCH, 512], BF16)
                bg = Buf()
                sq = gT[:, 0:8, :]
                rs_t = sb("rs_t", [128, 512], F32)
                rstd = sb("rstd", [128, 512], F32)
                brs = Buf()
                sa = [sb(f"sa{i}", [128, 512], F32) for i in range(2)]
                bsa = [Buf() for _ in range(2)]
                wins = [Slot(S, sb(f"win{i}", [128, 8, 2, 512], BF16), f"win{i}") for i in range(2)]
                wouts = [Slot(S, sb(f"wout{i}", [128, NCH, 128], BF16), f"wout{i}") for i in range(3)]
                nwin = [0]
                nwout = [0]
                ps_ss = pst("ps_ss")
                bss = Buf()
                ps_a = [pst(f"ps_a{i}") for i in range(2)]
                ps_b = [pst(f"ps_b{i}") for i in range(2)]
                ps_o = [pst(f"ps_o{i}") for i in range(2)]
                bpa = [Buf() for _ in range(2)]
                bpb = [Buf() for _ in range(2)]
                bpo = [Buf() for _ in range(2)]
                npo = [0]
                if kind != "first":
                    wo = sb("wo", [128, 8, 8, 128], BF16)
                    bwo = Buf()
                    S.op("sp", lambda e: e.dma_start(out=wo[:].rearrange("p a b c -> p (a b c)"), in_=w_mo_b[l]),
                         reads=[b_wmo[l]], writes=[bwo], dsem=S.dsem("wo"))
                    yTs = [Slot(S, sb(f"yT{i}", [128, 8, 512], BF16), f"yT{i}") for i in range(2)]
                if kind in ("first", "last"):
                    xtok = sb("xtok", [128, 4, 1024], F32)
                    bxt = Buf()
                    xtsem = S.dsem("xtok")
                if kind != "last":
                    ln = l if kind == "first" else l + 1
                    hyb = (ln % 2 == 0)
                    stA = [sb(f"stA{i}", [128, 8, 512], BF16) for i in range(2)]
                    bstA = [Buf() for _ in range(2)]
                    stAsem = [S.dsem(f"stA{i}") for i in range(2)]
                    vst = sb("vst", [128, 16, 4, 64], BF16)
                    bvst = Buf()
                    vsem = S.dsem("vst")
                    if hyb:
                        stF = sb("stF", [128, 8, 512], F32)
                        bstF = Buf()
                        stFsem = S.dsem("stF")
                        fst = sb("fst", [8, 3, 512], F32)
                        bfst = Buf()
                        fsem = S.dsem("fst")
                        qn_t = sb("qn_t", [128, 512], F32)
                        qn_r = sb("qn_r", [128, 512], F32)
                        bqn = Buf()
                        sqq = sb("sqq", [128, 512], BF16)
                        bsqq = Buf()

                def load_win(src_ap, src_buf, ncols=512):
                    sl = wins[nwin[0] % len(wins)]
                    nwin[0] += 1
                    if ncols == 512:
                        S.op("sp", lambda e, sl=sl: e.dma_start(out=sl.ap[:].rearrange("p a b c -> p (a b c)"), in_=src_ap),
                             reads=[src_buf], writes=[sl.buf], dsem=sl.sem)
                    else:
                        S.op("sp", lambda e, sl=sl: e.dma_start(
                            out=sl.ap[:, :, :, 0:ncols],
                            in_=src_ap.rearrange("p (a b c) -> p a b c", a=8, b=2)[:, :, :, 0:ncols]),
                            reads=[src_buf], writes=[sl.buf], dsem=sl.sem)
                    return sl

                def load_wout(src_ap, src_buf):
                    sl = wouts[nwout[0] % len(wouts)]
                    nwout[0] += 1
                    S.op("sp", lambda e, sl=sl: e.dma_start(out=sl.ap[:].rearrange("p a c -> p (a c)"), in_=src_ap),
                         reads=[src_buf], writes=[sl.buf], dsem=sl.sem)
                    return sl

                def rmsnorm(gcol):
                    S.op("act", lambda e: e.activation(out=sq.rearrange("p a c -> p (a c)"),
                                                       in_=xT[:].rearrange("p a c -> p (a c)"), func=AF.Square),
                         reads=[bx], writes=[bg])
                    for k in range(8):
                        S.op("pe", lambda e, k=k: e.matmul(ps_ss[:], ones_b, sq[:, k, :], start=(k == 0), stop=(k == 7)),
                             reads=[bg, b_const], writes=[bss])
                    S.op("act", lambda e: e.activation(out=rs_t[:], in_=ps_ss[:], func=AF.Sqrt, bias=RMS_EPS, scale=1.0 / D),
                         reads=[bss], writes=[brs])
                    S.op("dve", lambda e: e.reciprocal(out=rstd[:], in_=rs_t[:]), reads=[brs], writes=[brs])
                    for k in range(8):
                        S.op("dve", lambda e, k=k: e.scalar_tensor_tensor(
                            out=hT[:, k, :], in0=xT[:, k, :], scalar=pp[:, gcol + k:gcol + k + 1], in1=rstd[:],
                            op0=ALU.mult, op1=ALU.mult), reads=[bx, brs, b_const], writes=[bh])

                def ffn(lf, i):
                    rmsnorm(cfg.c_ffn + (lf * 2 + i) * 8)
                    cnt = 0
                    for s_ in range(6):
                        ncols = 512 if s_ < 5 else 256
                        sl = load_win(w_in_b[lf, i, s_], b_win[lf][i][s_], ncols)
                        for j in range(ncols // 128):
                            c = s_ * 4 + j
                            pa, pb = ps_a[cnt % 2], ps_b[cnt % 2]
                            ba, bb = bpa[cnt % 2], bpb[cnt % 2]
                            sat, bsat = sa[cnt % 2], bsa[cnt % 2]
                            cnt += 1
                            for k in range(8):
                                S.op("pe", lambda e, sl=sl, k=k, j=j, pa=pa: e.matmul(
                                    pa[:], sl.ap[:, k, 0, j * 128:(j + 1) * 128], hT[:, k, :], start=(k == 0), stop=(k == 7)),
                                    reads=[sl.buf, bh], writes=[ba])
                            for k in range(8):
                                S.op("pe", lambda e, sl=sl, k=k, j=j, pb=pb: e.matmul(
                                    pb[:], sl.ap[:, k, 1, j * 128:(j + 1) * 128], hT[:, k, :], start=(k == 0), stop=(k == 7)),
                                    reads=[sl.buf, bh], writes=[bb])
                            S.op("act", lambda e, pa=pa, sat=sat: e.activation(out=sat[:], in_=pa[:], func=AF.Silu),
                                 reads=[ba], writes=[bsat])
                            S.op("dve", lambda e, c=c, pb=pb, sat=sat: e.tensor_tensor(
                                out=gT[:, c, :], in0=sat[:], in1=pb[:], op=ALU.mult), reads=[bsat, bb], writes=[bg])
                    for m in range(8):
                        sl = load_wout(w_out_b[lf, i, m], b_wout[lf][i][m])
                        po, bo = ps_o[npo[0] % 2], bpo[npo[0] % 2]
                        npo[0] += 1
                        for f in range(NCH):
                            S.op("pe", lambda e, sl=sl, f=f, po=po: e.matmul(
                                po[:], sl.ap[:, f, :], gT[:, f, :], start=(f == 0), stop=(f == NCH - 1)),
                                reads=[sl.buf, bg], writes=[bo])
                        S.op("dve", lambda e, m=m, po=po: e.scalar_tensor_tensor(
                            out=xT[:, m, :], in0=po[:], scalar=0.5, in1=xT[:, m, :], op0=ALU.mult, op1=ALU.add),
                            reads=[bo, bx], writes=[bx])

                def store(eng, dst, src, rbufs, mw, dsem):
                    S.op(eng, lambda e: e.dma_start(out=dst, in_=src), reads=rbufs, mwrites=mw, dsem=dsem)

                def xstore_rows(xb, r0, st, nchunk, t0, bst, ssem, mw=None):
                    per = max(1, min(nchunk, xb.CR // 128))
                    for c0 in range(0, nchunk, per):
                        dst = xb.srows(r0 + c0 * 128, per * 128)[:, t0:t0 + 512].rearrange("(c p) t -> p c t", p=128)
                        store("act", dst, st[:, c0:c0 + per, :], [bst], [b_sndb if mw is None else mw], ssem)

                def xstore_v(xb, voff_, nh, tt):
                    per = max(1, min(nh, xb.CR // 64))
                    for h0 in range(0, nh, per):
                        vv = xb.srows(voff_ + h0 * 64, per * 64).rearrange("a (two c) -> (a two) c", two=2)
                        vv = vv.rearrange("(h p) (b d) -> p h b d", p=128, d=64)
                        store("act", vv[:, :, tt * 4:(tt + 1) * 4, :], vst[:, h0:h0 + per, :, :], [bvst], [b_sndb], vsem)

                def proj_sb(tt):
                    t0 = tt * 512
                    rmsnorm(cfg.c_mix + ln * 8)
                    for which in range(2):
                        sl = load_win(w_mi_b[ln, which], b_wmi[ln][which])
                        st, bst, ssem = stA[which], bstA[which], stAsem[which]
                        for c in range(8):
                            pa, ba = ps_a[c % 2], bpa[c % 2]
                            for k in range(8):
                                S.op("pe", lambda e, sl=sl, k=k, c=c, pa=pa: e.matmul(
                                    pa[:], sl.ap[:, k, c // 4, (c % 4) * 128:(c % 4 + 1) * 128], hT[:, k, :],
                                    start=(k == 0), stop=(k == 7)), reads=[sl.buf, bh], writes=[ba])
                            S.op("act", lambda e, c=c, pa=pa, st=st: e.activation(out=st[:, c, :], in_=pa[:], func=AF.Copy),
                                 reads=[ba], writes=[bst])
                        if which == 0:
                            dst = qT_d[:, t0:t0 + 512].rearrange("(c p) t -> p c t", p=128)
                            store("act", dst, st[:], [bst], [b_qT], ssem)
                        else:
                            xstore_rows(xb_sb, 0, st, 8, t0, bst, ssem)
                    sl = load_win(w_mi_b[ln, 2], b_wmi[ln][2])
                    n = 0
                    for j in range(4):
                        for half in range(2):
                            pb, bb = ps_b[n % 2], bpb[n % 2]
                            n += 1
                            for k in range(8):
                                S.op("pe", lambda e, sl=sl, k=k, j=j, half=half, pb=pb: e.matmul(
                                    pb[:], hT[:, k, j * 128:(j + 1) * 128], sl.ap[:, k, half, :],
                                    start=(k == 0), stop=(k == 7)), reads=[sl.buf, bh], writes=[bb])
                            S.op("dve", lambda e, j=j, half=half, pb=pb: e.tensor_copy(
                                out=vst[:, half * 8:(half + 1) * 8, j, :], in_=pb[:].rearrange("p (h d) -> p h d", d=64)),
                                reads=[bb], writes=[bvst])
                    xstore_v(xb_sb, 1024, 16, tt)

                def proj_hy(tt):
                    t0 = tt * 512
                    e_ = ln // 2
                    rmsnorm(cfg.c_mix + ln * 8)
                    sl = load_win(w_mi_b[ln, 0], b_wmi[ln][0])
                    for c in range(8):
                        pa, ba = ps_a[c % 2], bpa[c % 2]
                        for k in range(8):
                            S.op("pe", lambda e, sl=sl, k=k, c=c, pa=pa: e.matmul(
                                pa[:], sl.ap[:, k, c // 4, (c % 4) * 128:(c % 4 + 1) * 128], hT[:, k, :],
                                start=(k == 0), stop=(k == 7)), reads=[sl.buf, bh], writes=[ba])
                        S.op("act", lambda e, c=c, pa=pa: e.activation(out=stF[:, c, :], in_=pa[:], func=AF.Copy),
                             reads=[ba], writes=[bstF])
                    xstore_rows(xb_f, 0, stF, 4, t0, bstF, stFsem, mw=b_sndf)
                    store("act", gate_d[:, t0:t0 + 512].rearrange("(c p) t -> p c t", p=128), stF[:, 4:8, :],
                          [bstF], [b_gate], stFsem)
                    sl = load_win(w_mi_b[ln, 1], b_wmi[ln][1])
                    for c in range(8):
                        which = c // 4
                        pa, ba = ps_a[c % 2], bpa[c % 2]
                        for k in range(8):
                            S.op("pe", lambda e, sl=sl, k=k, c=c, pa=pa: e.matmul(
                                pa[:], sl.ap[:, k, c // 4, (c % 4) * 128:(c % 4 + 1) * 128], hT[:, k, :],
                                start=(k == 0), stop=(k == 7)), reads=[sl.buf, bh], writes=[ba])
                        S.op("act", lambda e, pa=pa: e.activation(out=sqq[:], in_=pa[:], func=AF.Square),
                             reads=[ba], writes=[bsqq])
                        S.op("pe", lambda e: e.matmul(ps_ss[:], bd_b, sqq[:], start=True, stop=True),
                             reads=[bsqq, b_const], writes=[bss])
                        S.op("act", lambda e: e.activation(out=qn_t[:], in_=ps_ss[:], func=AF.Sqrt, bias=RMS_EPS, scale=1.0 / 64),
                             reads=[bss], writes=[bqn])
                        S.op("dve", lambda e: e.reciprocal(out=qn_r[:], in_=qn_t[:]), reads=[bqn], writes=[bqn])
                        gc = cfg.c_qk + e_ * 2 + which
                        S.op("dve", lambda e, c=c, pa=pa, gc=gc: e.scalar_tensor_tensor(
                            out=stA[c // 4][:, c % 4, :], in0=pa[:], scalar=pp[:, gc:gc + 1], in1=qn_r[:],
                            op0=ALU.mult, op1=ALU.mult), reads=[ba, bqn, b_const], writes=[bstA[c // 4]])
                        if c == 3:
                            store("act", qT_d[0:512, t0:t0 + 512].rearrange("(c p) t -> p c t", p=128), stA[0][:, 0:4, :],
                                  [bstA[0]], [b_qT], stAsem[0])
                        if c == 7:
                            xstore_rows(xb_hb, 0, stA[1], 4, t0, bstA[1], stAsem[1])
                    sl = load_win(w_mi_b[ln, 2], b_wmi[ln][2])
                    for j in range(4):
                        pb, bb = ps_b[j % 2], bpb[j % 2]
                        for k in range(8):
                            S.op("pe", lambda e, sl=sl, k=k, j=j, pb=pb: e.matmul(
                                pb[:], hT[:, k, j * 128:(j + 1) * 128], sl.ap[:, k, 0, :],
                                start=(k == 0), stop=(k == 7)), reads=[sl.buf, bh], writes=[bb])
                        S.op("dve", lambda e, j=j, pb=pb: e.tensor_copy(
                            out=vst[:, 0:8, j, :], in_=pb[:].rearrange("p (h d) -> p h d", d=64)),
                            reads=[bb], writes=[bvst])
                    xstore_v(xb_hb, 512, 8, tt)
                    pa, ba = ps_a[0], bpa[0]
                    for k in range(8):
                        S.op("pe", lambda e, sl=sl, k=k, pa=pa: e.matmul(
                            pa[0:8, :], sl.ap[:, k, 1, 0:8], hT[:, k, :], start=(k == 0), stop=(k == 7)),
                            reads=[sl.buf, bh], writes=[ba])
                    S.op("act", lambda e, pa=pa: e.activation(out=fst[:, 0, :], in_=pa[0:8, :], func=AF.Exp,
                                                              bias=nfb[0:8, e_:e_ + 1], scale=-1.0),
                         reads=[ba, b_const], writes=[bfst])
                    S.op("act", lambda e: e.activation(out=fst[:, 1, :], in_=fst[:, 0, :], func=AF.Ln, bias=1.0),
                         reads=[bfst], writes=[bfst])
                    S.op("dve", lambda e: e.tensor_scalar(out=fst[:, 2, :], in0=fst[:, 1, :], scalar1=-1.0, scalar2=None,
                                                          op0=ALU.mult), reads=[bfst], writes=[bfst])
                    store("act", snd_lv[:, t0:t0 + 512], fst[:, 2, :], [bfst], [b_sndf], fsem)

                for tt in range(NG):
                    t0 = tt * 512
                    if kind == "first":
                        S.op("sp", lambda e, t0=t0: e.dma_start(
                            out=xtok[:], in_=x_in[t0:t0 + 512, :].rearrange("(j p) f -> p j f", p=128)),
                            writes=[bxt], dsem=xtsem)
                        for k in range(8):
                            po, bo = ps_o[npo[0] % 2], bpo[npo[0] % 2]
                            npo[0] += 1
                            for j in range(4):
                                S.op("pe", lambda e, k=k, j=j, po=po: e.transpose(
                                    po[:, j * 128:(j + 1) * 128], xtok[:, j, k * 128:(k + 1) * 128], ident_f),
                                    reads=[bxt, b_const], writes=[bo])
                            S.op("dve", lambda e, k=k, po=po: e.tensor_copy(out=xT[:, k, :], in_=po[:]),
                                 reads=[bo], writes=[bx])
                        ffn(0, 0)
                    else:
                        ysl = yTs[tt % 2]
                        S.op("sp", lambda e, t0=t0, ysl=ysl: e.dma_start(
                            out=ysl.ap[:], in_=y_d[:, t0:t0 + 512].rearrange("(c p) t -> p c t", p=128)),
                            reads=[b_y[tt]], writes=[ysl.buf], dsem=ysl.sem)
                        S.op("sp", lambda e, t0=t0: e.dma_start(
                            out=xT[:], in_=xT_d[:, :, t0:t0 + 512].rearrange("c p t -> p c t")),
                            reads=[b_xT[tt]], writes=[bx], dsem=xsem)
                        for m in range(8):
                            po, bo = ps_o[npo[0] % 2], bpo[npo[0] % 2]
                            npo[0] += 1
                            for f in range(8):
                                S.op("pe", lambda e, m=m, f=f, po=po, ysl=ysl: e.matmul(
                                    po[:], wo[:, m, f, :], ysl.ap[:, f, :], start=(f == 0), stop=(f == 7)),
                                    reads=[bwo, ysl.buf], writes=[bo])
                            S.op("dve", lambda e, m=m, po=po: e.tensor_tensor(
                                out=xT[:, m, :], in0=po[:], in1=xT[:, m, :], op=ALU.add), reads=[bo, bx], writes=[bx])
                        ffn(l, 1)
                        if kind == "mid":
                            ffn(l + 1, 0)
                    if kind != "last":
                        S.op("act", lambda e, t0=t0: e.dma_start(
                            out=xT_d[:, :, t0:t0 + 512].rearrange("c p t -> p c t"), in_=xT[:]),
                            reads=[bx], writes=[b_xT[tt]], dsem=xsem)
                        if hyb:
                            proj_hy(tt)
                        else:
                            proj_sb(tt)
                    else:
                        for j in range(4):
                            for kk in range(2):
                                po, bo = ps_o[npo[0] % 2], bpo[npo[0] % 2]
                                npo[0] += 1
                                for k4 in range(4):
                                    k = kk * 4 + k4
                                    S.op("pe", lambda e, k=k, k4=k4, j=j, po=po: e.transpose(
                                        po[:, k4 * 128:(k4 + 1) * 128], xT[:, k, j * 128:(j + 1) * 128], ident_f),
                                        reads=[bx, b_const], writes=[bo])
                                S.op("dve", lambda e, kk=kk, j=j, po=po: e.tensor_copy(
                                    out=xtok[:, j, kk * 512:(kk + 1) * 512], in_=po[:]), reads=[bo], writes=[bxt])
                        S.op("act", lambda e, t0=t0: e.dma_start(
                            out=out_d[t0:t0 + 512, :].rearrange("(j p) f -> p j f", p=128), in_=xtok[:]),
                            reads=[bxt], mwrites=[b_out], dsem=xtsem)
                S.emit()

        b_rcvl = Buf()

        def exchange(hyb):
            def ag(src, dst, rb, wb):
                S.op("pool", lambda e: e.collective_compute("AllGather", ALU.bypass, replica_groups=RG,
                                                            ins=[src.ap().opt()], outs=[dst.ap().opt()]),
                     reads=[rb], writes=[wb], dsem=cc_sem, inc=1)

            xb = xb_hb if hyb else xb_sb
            if hyb:
                for c in range(xb_f.n):
                    ag(xb_f.snd[c], xb_f.rcv[c], b_sndf, xb_f.rbuf[c])
                ag(snd_l, rcv_l, b_sndf, b_rcvl)
            hn = xb.n // 2
            order = [x for c in range(hn) for x in (c, hn + c)] if xb.n >= 2 else list(range(xb.n))
            for c in order:
                ag(xb.snd[c], xb.rcv[c], b_sndb, xb.rbuf[c])

        def attn_phase(l):
            hyb = (l % 2 == 0)
            e_ = l // 2
            with ExitStack() as ps:
                uid[0] += 1
                u_ = uid[0]

                def sb(name, shape, dt, u_=u_):
                    return ps.enter_context(nc.sbuf_tensor(f"s{u_}_{name}", list(shape), dt))

                def pst(name, shape=(128, 512), dt=F32, u_=u_):
                    return ps.enter_context(nc.psum_tensor(f"p{u_}_{name}", list(shape), dt))

                S.barrier()
                pp = sb("pp", [128, cfg.NPP], F32)
                cb = sb("cb", [128, 640], BF16)
                cf = sb("cf", [128, 256], F32)
                mk = sb("mk", [128, 4, 4, 512], BF16)
                b_const = Buf()
                cs = S.dsem("at_const")
                S.op("sp", lambda e: e.dma_start(out=pp[:], in_=pp_d[:, :]), writes=[b_const], dsem=cs)
                S.op("sp", lambda e: e.dma_start(out=cb[:], in_=cb_d[:, :]), mwrites=[b_const], dsem=cs)
                S.op("sp", lambda e: e.dma_start(out=cf[:], in_=cf_d[:, :]), mwrites=[b_const], dsem=cs)
                S.op("sp", lambda e: e.dma_start(out=mk[:].rearrange("p a b c -> p (a b c)"), in_=mk_d[:, :]),
                     mwrites=[b_const], dsem=cs)
                exchange(hyb)
                ident_b = cb[:, 0:128]
                negT8 = cb[:, 256:384]
                negones8 = cb[:, 384:512]
                ident_f = cf[:, 0:128]
                ones_f = cf[:, 128:256]
                sel0 = pp[:, cfg.c_sel:cfg.c_sel + 1]
                sel1 = pp[:, cfg.c_sel + 1:cfg.c_sel + 2]

                KD = 65 if hyb else 64
                VD = 65 if hyb else 64
                nheads = 8 if hyb else 16
                voff = 512 if hyb else 1024
                half = 1024 if hyb else 2048
                xb = xb_hb if hyb else xb_sb
                KT = [Slot(S, sb(f"KT{i}", [128, S_], BF16), f"KT{i}") for i in range(2)]
                VV = [Slot(S, sb(f"V{i}", [128, NB, VD], BF16), f"V{i}") for i in range(2)]
                QT = [Slot(S, sb(f"QT{i}", [128, TL], BF16), f"QT{i}") for i in range(2)]
                psA = [pst(f"psA{i}") for i in range(2)]
                bA = [Buf() for _ in range(2)]
                psO = [pst(f"psO{i}") for i in range(2)]
                bO = [Buf() for _ in range(2)]
                Wt = [sb(f"W{i}", [128, 512], BF16) for i in range(3)]
                bW = [Buf() for _ in range(3)]
                yst = [sb(f"yst{i}", [64, 512], BF16) for i in range(2)]
                byst = [Buf() for _ in range(2)]
                ystsem = [S.dsem(f"yst{i}") for i in range(2)]
                if hyb:
                    ps_m = pst("ps_m")
                    bpm = Buf()
                    for i in range(2):
                        S.op("pool", lambda e, i=i: e.memset(KT[i].ap[64:65, :], 1.0), writes=[KT[i].buf])
                        S.op("pool", lambda e, i=i: e.memset(VV[i].ap[:, :, 64:65], 1.0), writes=[VV[i].buf])
                    cw = sb("cw", [128, S_], F32)
                    bcw = Buf()
                    cwsem = S.dsem("cw")
                    cown = sb("cown", [128, TL], F32)
                    bco = Buf()
                    cT = sb("cT", [128, NB], F32)
                    grefB = sb("grefB", [128, NG], F32)
                    bcT = Buf()
                    biasG = [sb(f"biasG{i}", [128, NB], F32) for i in range(2)]
                    bbias = [Buf() for _ in range(2)]
                    osb = sb("osb", [64, 512], F32)
                    rec = sb("rec", [128, 512], F32)
                    bos = Buf()
                else:
                    psB = [pst(f"psB{i}") for i in range(2)]
                    bB = [Buf() for _ in range(2)]
                    Et = [sb(f"E{i}", [128, 512], F32) for i in range(2)]
                    bE = [Buf() for _ in range(2)]
                    SPt = [sb(f"SP{i}", [128, 512], BF16) for i in range(3)]
                    bSP = [Buf() for _ in range(3)]
                    Rt = [sb(f"R{i}", [128, 512], BF16) for i in range(3)]
                    bR = [Buf() for _ in range(3)]

                def load_head(h):
                    k, v, q = KT[h % 2], VV[h % 2], QT[h % 2]
                    for r in range(2):
                        src = xb.rrows(r, h * 64, 64).rearrange("p (g w) -> p g w", w=512)
                        dst = k.ap[0:64, :].rearrange("p (g r w) -> p g r w", r=2, w=512)[:, :, r, :]
                        S.op("sp", lambda e, src=src, dst=dst: e.dma_start(out=dst, in_=src),
                             reads=[xb.rb(h * 64)], **({"writes": [k.buf]} if r == 0 else {"mwrites": [k.buf]}), dsem=k.sem)
                        vv = xb.rrows(r, voff + h * 64, 64).rearrange("a (two c) -> (a two) c", two=2)
                        vv = vv.rearrange("p (g j d) -> p g j d", d=64, j=4)
                        dstv = v.ap[:, :, 0:64].rearrange("p (g r j) d -> p g r j d", r=2, j=4)[:, :, r, :, :]
                        if hyb:
                            for g_ in range(NG):
                                S.op("sp", lambda e, vv=vv, dstv=dstv, h=h, g_=g_: e.dma_start(out=dstv[:, g_], in_=vv[:, g_]),
                                     reads=[xb.rb(voff + h * 64)], **({"writes": [v.buf]} if (r == 0 and g_ == 0) else {"mwrites": [v.buf]}),
                                     dsem=v.sem)
                        else:
                            S.op("sp", lambda e, vv=vv, dstv=dstv, h=h: e.dma_start(out=dstv, in_=vv),
                                 reads=[xb.rb(voff + h * 64)], **({"writes": [v.buf]} if r == 0 else {"mwrites": [v.buf]}), dsem=v.sem)
                    S.op("sp", lambda e, h=h, q=q: e.dma_start(out=q.ap[0:64, :], in_=qT_d[h * 64:(h + 1) * 64, :]),
                         reads=[b_qT], writes=[q.buf], dsem=q.sem)

                if hyb:
                    rglru(l, ps, sb, pst, pp, cb, cf, b_const, sel0, sel1)

                tiles = []
                for h in range(nheads):
                    for i in range(NG):
                        lst = [(kg, j) for kg in range(2 * i + 1, -1, -1) for j in range(3, -1, -1)]
                        for n, (kg, j) in enumerate(lst):
                            mkind = None
                            if kg == 2 * i + 1:
                                mkind = 1
                            elif kg == 2 * i:
                                mkind = 0
                            tiles.append(dict(h=h, i=i, kb=kg * 4 + j, j=j, mk=mkind, first=(n == 0),
                                              last=(n == len(lst) - 1), g=h * NG + i))
                NT = len(tiles)

                def head_prep(h):
                    q = QT[h % 2]
                    for r in range(2):
                        src = rcv_lv[r, h:h + 1, :].rearrange("p (g w) -> p g w", w=512)
                        dst = cw[64:65, :].rearrange("p (g r w) -> p g r w", r=2, w=512)[:, :, r, :]
                        S.op("sp", lambda e, src=src, dst=dst: e.dma_start(out=dst, in_=src), reads=[b_rcvl],
                             **({"writes": [bcw]} if r == 0 else {"mwrites": [bcw]}), dsem=cwsem)
                    S.op("dve", lambda e: e.tensor_tensor_scan(out=cw[64:65, :], data0=cw[64:65, :], data1=cw[64:65, :],
                                                               initial=0.0, op0=ALU.add, op1=ALU.min),
                         reads=[bcw], writes=[bcw])
                    cwv = cw[64:65, :].rearrange("p (g r w) -> p g r w", r=2, w=512)
                    S.op("dve", lambda e: e.tensor_scalar(out=cown[64:65, :].rearrange("p (g w) -> p g w", w=512),
                                                          in0=cwv[:, :, 0, :], scalar1=sel0[64:65, :], scalar2=None,
                                                          op0=ALU.mult), reads=[bcw, b_const], writes=[bco])
                    S.op("dve", lambda e: e.scalar_tensor_tensor(
                        out=cown[64:65, :].rearrange("p (g w) -> p g w", w=512), in0=cwv[:, :, 1, :], scalar=sel1[64:65, :],
                        in1=cown[64:65, :].rearrange("p (g w) -> p g w", w=512), op0=ALU.mult, op1=ALU.add),
                        reads=[bcw, b_const, bco], writes=[bco])
                    for i in range(NG):
                        S.op("dve", lambda e, i=i, q=q: e.tensor_scalar(
                            out=q.ap[64:65, i * 512:(i + 1) * 512], in0=cown[64:65, i * 512:(i + 1) * 512],
                            scalar1=cown[64:65, i * 512:i * 512 + 1], scalar2=8.0, op0=ALU.subtract, op1=ALU.mult),
                            reads=[bco], mwrites=[q.buf])
                    for blk in range(NB):
                        S.op("pe", lambda e, blk=blk: e.matmul(ps_m[:, 2 * blk:2 * blk + 2], cw[64:65, blk * 128:(blk + 1) * 128],
                                                               ones_f[64:65, 0:2], start=True, stop=True),
                             reads=[bcw, b_const], writes=[bpm])
                    S.op("dve", lambda e: e.tensor_copy(out=cT[:], in_=ps_m[:, 0:2 * NB].rearrange("p (b two) -> p b two", two=2)[:, :, 0]),
                         reads=[bpm], writes=[bcT])
                    S.op("pe", lambda e: e.matmul(ps_m[:, 256:256 + NG], ones_f[64:65, :],
                                                  cown[64:65, :].rearrange("p (g w) -> p g w", w=512)[:, :, 0],
                                                  start=True, stop=True), reads=[bco, b_const, bcT], writes=[bpm])
                    S.op("dve", lambda e: e.tensor_copy(out=grefB[:], in_=ps_m[:, 256:256 + NG]), reads=[bpm], writes=[bcT])

                def group_bias(g, i):
                    S.op("dve", lambda e: e.tensor_scalar(out=biasG[g % 2][:], in0=cT[:], scalar1=-1.0,
                                                          scalar2=grefB[:, i:i + 1], op0=ALU.mult, op1=ALU.add),
                         reads=[bcT], writes=[bbias[g % 2]])

                def emitA(s):
                    t = tiles[s]
                    h, i, kb = t["h"], t["i"], t["kb"]
                    k, q = KT[h % 2], QT[h % 2]
                    pa, ba = psA[s % 2], bA[s % 2]
                    S.op("pe", lambda e: e.matmul(pa[:], k.ap[0:KD, kb * 128:(kb + 1) * 128], q.ap[0:KD, i * 512:(i + 1) * 512],
                                                  start=True, stop=(t["mk"] is None)), reads=[k.buf, q.buf], writes=[ba])
                    if t["mk"] is not None:
                        mi = (0 if hyb else 2) + t["mk"]
                        S.op("pe", lambda e: e.matmul(pa[:], ident_b, mk[:, mi, t["j"], :], start=False, stop=True),
                             reads=[b_const], writes=[ba])

                def evac_group(t):
                    g, h, i = t["g"], t["h"], t["i"]
                    po, bo = psO[g % 2], bO[g % 2]
                    ys, bys, yss = yst[g % 2], byst[g % 2], ystsem[g % 2]
                    if hyb:
                        S.op("dve", lambda e: e.tensor_copy(out=osb[:], in_=po[0:64, :]), reads=[bo], writes=[bos])
                        S.op("dve", lambda e: e.reciprocal(out=rec[64:65, :], in_=po[64:65, :]), reads=[bo], writes=[bos])
                        S.op("pe", lambda e: e.matmul(ps_m[0:64, :], ones_f[64:65, 0:64], rec[64:65, :], start=True, stop=True),
                             reads=[bos, b_const, bcT], writes=[bpm])
                        S.op("dve", lambda e: e.tensor_tensor(out=ys[:], in0=osb[:], in1=ps_m[0:64, :], op=ALU.mult),
                             reads=[bos, bpm], writes=[bys])
                        row = 512 + h * 64
                    else:
                        S.op("dve", lambda e: e.tensor_copy(out=ys[:], in_=po[0:64, :]), reads=[bo], writes=[bys])
                        row = h * 64
                    S.op("sp", lambda e: e.dma_start(out=y_d[row:row + 64, i * 512:(i + 1) * 512], in_=ys[:]),
                         reads=[bys], mwrites=[b_y[i]], dsem=yss)

                load_head(0)
                if hyb:
                    head_prep(0)
                emitA(0)
                for s in range(NT):
                    t = tiles[s]
                    h, i, kb, g = t["h"], t["i"], t["kb"], t["g"]
                    k, v, q = KT[h % 2], VV[h % 2], QT[h % 2]
                    if t["first"] and i == 0 and h + 1 < nheads:
                        load_head(h + 1)
                    if hyb and t["first"]:
                        group_bias(g, i)
                    if s + 1 < NT:
                        if hyb and tiles[s + 1]["h"] != h:
                            head_prep(h + 1)
                        emitA(s + 1)
                    pa, ba = psA[s % 2], bA[s % 2]
                    if hyb:
                        w, bw = Wt[s % 3], bW[s % 3]
                        S.op("act", lambda e, pa=pa, w=w, g=g, kb=kb: e.activation(
                            out=w[:], in_=pa[:], func=AF.Exp, bias=biasG[g % 2][:, kb:kb + 1], scale=0.125),
                            reads=[ba, bbias[g % 2]], writes=[bw])
                        po, bo = psO[g % 2], bO[g % 2]
                        S.op("pe", lambda e, po=po, v=v, w=w, kb=kb, t=t: e.matmul(
                            po[0:65, :], v.ap[:, kb, :], w[:], start=t["first"], stop=t["last"]),
                            reads=[v.buf, bw], writes=[bo])
                        if t["last"]:
                            evac_group(t)
                    else:
                        et, bet = Et[s % 2], bE[s % 2]
                        sp_, bsp = SPt[s % 3], bSP[s % 3]
                        S.op("act", lambda e, pa=pa, et=et: e.activation(out=et[:], in_=pa[:], func=AF.Exp, scale=0.125),
                             reads=[ba], writes=[bet])
                        S.op("act", lambda e, et=et, sp_=sp_: e.activation(out=sp_[:], in_=et[:], func=AF.Ln, bias=1.0),
                             reads=[bet], writes=[bsp])
                        rn, brn = Rt[s % 3], bR[s % 3]
                        ro, bro = Rt[(s - 1) % 3], bR[(s - 1) % 3]
                        if t["first"]:
                            S.op("pool", lambda e, rn=rn, sp_=sp_: e.tensor_copy(out=rn[:], in_=sp_[:]), reads=[bsp], writes=[brn])
                        else:
                            S.op("pool", lambda e, rn=rn, ro=ro, sp_=sp_: e.tensor_tensor(out=rn[:], in0=ro[:], in1=sp_[:], op=ALU.add),
                                 reads=[bsp, bro], writes=[brn])
                        pb, bb = psB[s % 2], bB[s % 2]
                        S.op("pe", lambda e, pb=pb, k=k, q=q, kb=kb, i=i: e.matmul(
                            pb[:], k.ap[0:64, kb * 128:(kb + 1) * 128], q.ap[0:64, i * 512:(i + 1) * 512], start=True, stop=False),
                            reads=[k.buf, q.buf], writes=[bb])
                        if t["mk"] is not None:
                            S.op("pe", lambda e, pb=pb, t=t: e.matmul(pb[:], ident_b, mk[:, 2 + t["mk"], t["j"], :], start=False, stop=False),
                                 reads=[b_const], writes=[bb])
                        S.op("pe", lambda e, pb=pb, sp_=sp_, t=t: e.matmul(pb[:], negT8, sp_[:], start=False, stop=t["first"]),
                             reads=[bsp, b_const], writes=[bb])
                        if not t["first"]:
                            S.op("pe", lambda e, pb=pb, ro=ro: e.matmul(pb[:], negones8, ro[:], start=False, stop=True),
                                 reads=[bro, b_const], writes=[bb])
                        if s >= 1:
                            sb_pv(s - 1, tiles, psB, bB, Wt, bW, psO, bO, VV, evac_group)
                if not hyb:
                    sb_pv(NT - 1, tiles, psB, bB, Wt, bW, psO, bO, VV, evac_group)
                S.emit()

        def sb_pv(s, tiles, psB, bB, Wt, bW, psO, bO, VV, evac_group):
            t = tiles[s]
            g, h, kb = t["g"], t["h"], t["kb"]
            pb, bb = psB[s % 2], bB[s % 2]
            w, bw = Wt[s % 3], bW[s % 3]
            v = VV[h % 2]
            S.op("act", lambda e: e.activation(out=w[:], in_=pb[:], func=AF.Exp, scale=0.125), reads=[bb], writes=[bw])
            po, bo = psO[g % 2], bO[g % 2]
            S.op("pe", lambda e: e.matmul(po[0:64, :], v.ap[:, kb, :], w[:], start=t["first"], stop=t["last"]),
                 reads=[v.buf, bw], writes=[bo])
            if t["last"]:
                evac_group(t)

        def rglru(l, ps, sb, pst, pp, cb, cf, b_const, sel0, sel1):
            e_ = l // 2
            SEG = min(1024, S_)
            NSEG = S_ // SEG
            OWN = SEG // 2
            gw = sb("gw", [128, 2, 4, 128], BF16)
            bgw = Buf()
            S.op("sp", lambda e: e.dma_start(out=gw[:].rearrange("p a b c -> p (a b c)"), in_=gw_b[e_]),
                 reads=[b_gw[e_]], writes=[bgw], dsem=S.dsem("gw"))
            rgd = sb("rgd", [128, 4, 4], F32)
            brgd = Buf()
            uS = [sb(f"uS{i}", [128, SEG + 3], F32) for i in range(2)]
            buS = [Buf() for _ in range(2)]
            usem = [S.dsem(f"uS{i}") for i in range(2)]
            xc = sb("xc", [128, SEG], F32)
            xcb = sb("xcb", [128, SEG], BF16)
            bxc = Buf()
            r_t = sb("r_t", [128, SEG], F32)
            i_t = sb("i_t", [128, SEG], F32)
            a_t = sb("a_t", [128, SEG], F32)
            s_t = sb("s_t", [128, SEG], F32)
            bri, ba_, bs_ = Buf(), Buf(), Buf()
            h_t = [sb(f"h_t{i}", [128, SEG], F32) for i in range(2)]
            bht = [Buf() for _ in range(2)]
            gt = sb("gt", [128, OWN], F32)
            bgt = Buf()
            gsem = S.dsem("gt")
            ho = sb("ho", [128, OWN], F32)
            bho = Buf()
            yo = sb("yo", [128, OWN], BF16)
            byo = Buf()
            yosem = S.dsem("yo")
            psg = [pst(f"psg{i}") for i in range(2)]
            bpg = [Buf() for _ in range(2)]
            npg = [0]
            nseg = 0
            for c in range(4):
                col = cfg.c_rg + (e_ * 4 + c) * 8
                S.op("act", lambda e, c=c, col=col: e.activation(out=rgd[:, c, 0:1], in_=pp[:, col + 7:col + 8], func=AF.Exp, scale=-1.0),
                     reads=[b_const], writes=[brgd])
                S.op("act", lambda e, c=c: e.activation(out=rgd[:, c, 1:2], in_=rgd[:, c, 0:1], func=AF.Ln, bias=1.0),
                     reads=[brgd], writes=[brgd])
                S.op("dve", lambda e, c=c: e.tensor_scalar(out=rgd[:, c, 2:3], in0=rgd[:, c, 1:2], scalar1=-8.0, scalar2=None, op0=ALU.mult),
                     reads=[brgd], writes=[brgd])
                S.op("dve", lambda e, c=c: e.tensor_scalar(out=rgd[:, c, 3:4], in0=rgd[:, c, 1:2], scalar1=-16.0, scalar2=None, op0=ALU.mult),
                     reads=[brgd], writes=[brgd])
                for sg in range(NSEG):
                    u, bu, us = uS[nseg % 2], buS[nseg % 2], usem[nseg % 2]
                    up, bup = uS[(nseg - 1) % 2], buS[(nseg - 1) % 2]
                    hcur, bhc = h_t[nseg % 2], bht[nseg % 2]
                    hprev, bhp = h_t[(nseg - 1) % 2], bht[(nseg - 1) % 2]
                    nseg += 1
                    g0 = sg * (SEG // 512)
                    for r in range(2):
                        src = xb_f.rrows(r, c * 128, 128)[:, (g0 // 2) * 512:(g0 // 2) * 512 + OWN]
                        src = src.rearrange("p (g w) -> p g w", w=512)
                        dst = u[:, 3:3 + SEG].rearrange("p (g r w) -> p g r w", r=2, w=512)[:, :, r, :]
                        S.op("sp", lambda e, src=src, dst=dst: e.dma_start(out=dst, in_=src), reads=[xb_f.rb(c * 128)],
                             **({"writes": [bu]} if r == 0 else {"mwrites": [bu]}), dsem=us)
                    S.op("sp", lambda e, c=c, sg=sg: e.dma_start(out=gt[:], in_=gate_d[c * 128:(c + 1) * 128, sg * OWN:(sg + 1) * OWN]),
                         reads=[b_gate], writes=[bgt], dsem=gsem)
                    if sg == 0:
                        S.op("pool", lambda e, u=u: e.memset(u[:, 0:3], 0.0), mwrites=[bu])
                    else:
                        S.op("pool", lambda e, u=u, up=up: e.tensor_copy(out=u[:, 0:3], in_=up[:, SEG:SEG + 3]),
                             reads=[bup], mwrites=[bu])
                    S.op("dve", lambda e, u=u, col=col: e.tensor_scalar(
                        out=xc[:], in0=u[:, 3:3 + SEG], scalar1=pp[:, col + 3:col + 4], scalar2=pp[:, col + 4:col + 5],
                        op0=ALU.mult, op1=ALU.add), reads=[bu, b_const], writes=[bxc])
                    for j in range(3):
                        S.op("dve", lambda e, u=u, col=col, j=j: e.scalar_tensor_tensor(
                            out=xc[:], in0=u[:, j:j + SEG], scalar=pp[:, col + j:col + j + 1], in1=xc[:],
                            op0=ALU.mult, op1=ALU.add), reads=[bu, b_const, bxc], writes=[bxc])
                    S.op("act", lambda e: e.activation(out=xcb[:], in_=xc[:], func=AF.Copy), reads=[bxc], writes=[bxc])
                    for qq in range(SEG // 512):
                        for gi, dst_t in ((0, r_t), (1, i_t)):
                            pg, bg_ = psg[npg[0] % 2], bpg[npg[0] % 2]
                            npg[0] += 1
                            S.op("pe", lambda e, pg=pg, gi=gi, c=c, qq=qq: e.matmul(
                                pg[:], gw[:, gi, c, :], xcb[:, qq * 512:(qq + 1) * 512], start=True, stop=True),
                                reads=[bgw, bxc], writes=[bg_])
                            S.op("act", lambda e, pg=pg, gi=gi, dst_t=dst_t, qq=qq, col=col: e.activation(
                                out=dst_t[:, qq * 512:(qq + 1) * 512], in_=pg[:], func=AF.Sigmoid,
                                bias=pp[:, col + 5 + gi:col + 6 + gi]), reads=[bg_, b_const], mwrites=[bri])
                    S.op("act", lambda e, c=c: e.activation(out=a_t[:], in_=r_t[:], func=AF.Exp, scale=rgd[:, c, 2:3]),
                         reads=[bri, brgd], writes=[ba_])
                    S.op("act", lambda e, c=c: e.activation(out=s_t[:], in_=r_t[:], func=AF.Exp, scale=rgd[:, c, 3:4]),
                         reads=[bri, brgd], writes=[bs_])
                    S.op("act", lambda e: e.activation(out=s_t[:], in_=s_t[:], func=AF.Sqrt, bias=1.0, scale=-1.0),
                         reads=[bs_], writes=[bs_])
                    S.op("dve", lambda e: e.tensor_tensor(out=i_t[:], in0=i_t[:], in1=xc[:], op=ALU.mult),
                         reads=[bri, bxc], writes=[bri])
                    S.op("dve", lambda e: e.tensor_tensor(out=s_t[:], in0=s_t[:], in1=i_t[:], op=ALU.mult),
                         reads=[bri, bs_], writes=[bs_])
                    init = 0.0 if sg == 0 else hprev[:, SEG - 1:SEG]
                    S.op("dve", lambda e, hcur=hcur, init=init: e.tensor_tensor_scan(
                        out=hcur[:], data0=a_t[:], data1=s_t[:], initial=init, op0=ALU.mult, op1=ALU.add),
                        reads=[ba_, bs_] + ([bhp] if sg > 0 else []), writes=[bhc])
                    hv = hcur[:].rearrange("p (g r w) -> p g r w", r=2, w=512)
                    S.op("dve", lambda e, hv=hv: e.tensor_scalar(out=ho[:].rearrange("p (g w) -> p g w", w=512), in0=hv[:, :, 0, :],
                                                                 scalar1=sel0, scalar2=None, op0=ALU.mult),
                         reads=[bhc, b_const], writes=[bho])
                    S.op("dve", lambda e, hv=hv: e.scalar_tensor_tensor(
                        out=ho[:].rearrange("p (g w) -> p g w", w=512), in0=hv[:, :, 1, :], scalar=sel1,
                        in1=ho[:].rearrange("p (g w) -> p g w", w=512), op0=ALU.mult, op1=ALU.add),
                        reads=[bhc, b_const, bho], writes=[bho])
                    S.op("act", lambda e: e.activation(out=gt[:], in_=gt[:], func=AF.Gelu_apprx_tanh), reads=[bgt], writes=[bgt])
                    S.op("dve", lambda e: e.tensor_tensor(out=yo[:], in0=ho[:], in1=gt[:], op=ALU.mult),
                         reads=[bho, bgt], writes=[byo])
                    for gg in range(OWN // 512):
                        i = sg * (OWN // 512) + gg
                        S.op("sp", lambda e, c=c, i=i, gg=gg: e.dma_start(
                            out=y_d[c * 128:(c + 1) * 128, i * 512:(i + 1) * 512], in_=yo[:, gg * 512:(gg + 1) * 512]),
                            reads=[byo], mwrites=[b_y[i]], dsem=yosem)

        cvt_ffn(0, 0)
        cvt_mix(0)
        cvt_ffn(0, 1)
        if depth > 1:
            cvt_ffn(1, 0)
        import os
        stop = int(os.environ.get("K_STOP", "99"))
        token_pass("first", 0)
        for l in range(depth):
            if stop <= 3 * l:
                break
            if stop <= 3 * l + 1:
                break
            if l + 1 < depth:
                cvt_mix(l + 1)
                cvt_ffn(l + 1, 1)
                if l + 2 < depth:
                    cvt_ffn(l + 2, 0)
            attn_phase(l)
            if stop <= 3 * l + 2:
                break
            token_pass("mid" if l + 1 < depth else "last", l)
        S.finish()
    return nc


def _arrange(cfg, inp):
    depth = cfg.depth
    f32 = np.float32
    w_in = np.asarray(inp["ffn_w_in"], f32)[:depth]
    a = np.zeros((depth, 2, 6, 128, 8, 2, 512), f32)
    wi = w_in.reshape(depth, 2, 8, 128, 2, DFF)
    for s in range(6):
        n = 512 if s < 5 else 256
        a[:, :, s, :, :, :, :n] = wi[:, :, :, :, :, s * 512:s * 512 + n].transpose(0, 1, 3, 2, 4, 5)
    w_in_r = a.reshape(depth, 2, 6, 128, 8 * 2 * 512)
    w_out = np.asarray(inp["ffn_w_out"], f32)[:depth]
    w_out_r = np.ascontiguousarray(w_out.reshape(depth, 2, NCH, 128, 8, 128).transpose(0, 1, 4, 3, 2, 5)).reshape(
        depth, 2, 8, 128, NCH * 128)
    w_mi = np.zeros((depth, 3, 128, 8, 2, 512), f32)
    w_mo = np.zeros((depth, 128, 8, 8, 128), f32)
    for l in range(depth):
        if l % 2 == 0:
            w = np.asarray(inp["hy_w_in"][l // 2], f32).reshape(8, 128, 2568)
            for s in range(2):
                for ab in range(2):
                    c0 = (s * 2 + ab) * 512
                    w_mi[l, s, :, :, ab, :] = w[:, :, c0:c0 + 512].transpose(1, 0, 2)
            w_mi[l, 2, :, :, 0, :] = w[:, :, 2048:2560].transpose(1, 0, 2)
            w_mi[l, 2, :, :, 1, 0:8] = w[:, :, 2560:2568].transpose(1, 0, 2)
            wo = np.asarray(inp["hy_w_out"][l // 2], f32)
        else:
            w = np.asarray(inp["sb_w_qkv"][l // 2], f32).reshape(8, 128, 3072)
            for s in range(3):
                for ab in range(2):
                    c0 = (s * 2 + ab) * 512
                    w_mi[l, s, :, :, ab, :] = w[:, :, c0:c0 + 512].transpose(1, 0, 2)
            wo = np.asarray(inp["sb_w_out"][l // 2], f32)
        w_mo[l] = wo.reshape(8, 128, 8, 128).transpose(1, 2, 0, 3)
    ne = max(1, cfg.n_even)
    gw = np.zeros((ne, 128, 2, 4, 128), f32)
    pp = np.zeros((128, cfg.NPP), f32)
    p = np.arange(128)
    for l in range(depth):
        for i in range(2):
            pp[:, cfg.c_ffn + (l * 2 + i) * 8:cfg.c_ffn + (l * 2 + i) * 8 + 8] = np.asarray(inp["ffn_norm"][l, i], f32).reshape(8, 128).T
        pp[:, cfg.c_mix + l * 8:cfg.c_mix + l * 8 + 8] = np.asarray(inp["mix_norm"][l], f32).reshape(8, 128).T
    for e in range(cfg.n_even):
        g = np.asarray(inp["rg_gate_w"][e], f32)
        for gi in range(2):
            for c in range(4):
                for hh in range(2):
                    gw[e, hh * 64:(hh + 1) * 64, gi, c, hh * 64:(hh + 1) * 64] = g[gi, c * 2 + hh]
        qk = np.asarray(inp["fox_qk_norm"][e], f32)
        pp[:, cfg.c_qk + e * 2 + 0] = qk[0][p % 64]
        pp[:, cfg.c_qk + e * 2 + 1] = qk[1][p % 64]
        for c in range(4):
            col = cfg.c_rg + (e * 4 + c) * 8
            for j in range(4):
                pp[:, col + j] = np.asarray(inp["rg_conv_w"][e, j], f32)[c * 128:(c + 1) * 128]
            pp[:, col + 4] = np.asarray(inp["rg_conv_b"][e], f32)[c * 128:(c + 1) * 128]
            pp[:, col + 5] = np.asarray(inp["rg_gate_b"][e, 0], f32)[c * 128:(c + 1) * 128]
            pp[:, col + 6] = np.asarray(inp["rg_gate_b"][e, 1], f32)[c * 128:(c + 1) * 128]
            pp[:, col + 7] = np.asarray(inp["rg_lambda"][e], f32)[c * 128:(c + 1) * 128]
        pp[0:8, cfg.c_fb + e] = np.asarray(inp["fox_fgate_b"][e], f32)
    return dict(w_in=w_in_r, w_out=w_out_r, w_mi=w_mi.reshape(depth, 3, 128, 8 * 2 * 512),
                w_mo=w_mo.reshape(depth, 128, 8 * 8 * 128), gate_w=gw.reshape(ne, 128, 2 * 4 * 128)), pp


def _consts(rank):
    bf = ml_dtypes.bfloat16
    cb = np.zeros((128, 640), np.float32)
    j = np.arange(128)[:, None]
    k = np.arange(128)[None, :]
    cb[:, 0:128] = np.eye(128)
    cb[:, 128:256] = 1.0
    cb[:, 256:384] = np.where(j >= k, -8.0, 0.0)
    cb[:, 384:512] = -8.0
    cb[:, 512:640] = (j // 64 == k // 64)
    mk = np.zeros((4, 4, 128, 512), np.float32)
    kk = np.arange(128)[:, None]
    q = np.arange(512)[None, :]
    for jb in range(4):
        kpos = jb * 128 + kk
        diag_fox = np.where(kpos > q, MASKV, 0.0)
        diag_sb = np.where(kpos >= q, MASKV, 0.0)
        full = np.full((128, 512), MASKV)
        zero = np.zeros((128, 512))
        if rank == 0:
            mk[0, jb], mk[1, jb], mk[2, jb], mk[3, jb] = diag_fox, full, diag_sb, full
        else:
            mk[0, jb], mk[1, jb], mk[2, jb], mk[3, jb] = zero, diag_fox, zero, diag_sb
    mk = np.ascontiguousarray(mk.transpose(2, 0, 1, 3)).reshape(128, 4 * 4 * 512)
    cf = np.zeros((128, 256), np.float32)
    cf[:, 0:128] = np.eye(128)
    cf[:, 128:256] = 1.0
    return cb.astype(bf), mk.astype(bf), cf


_PROG_CACHE = {}


def run(cfg, inp):
    x = np.asarray(inp["x"], np.float32)
    B = x.shape[0]
    assert B == cfg.npairs
    key = (cfg.S, cfg.depth, cfg.npairs)
    if key not in _PROG_CACHE:
        _PROG_CACHE[key] = build_program(cfg)
    nc = _PROG_CACHE[key]
    wts, pp = _arrange(cfg, inp)
    in_maps = []
    for core in range(2 * B):
        b, r = core // 2, core % 2
        xs = np.ascontiguousarray(x[b].reshape(cfg.NGG, 512, D)[r::2].reshape(cfg.TL, D))
        cb, mk, cf = _consts(r)
        ppc = pp.copy()
        ppc[:, cfg.c_sel] = 1.0 - r
        ppc[:, cfg.c_sel + 1] = float(r)
        m = dict(wts)
        m.update(x=xs, pp=ppc, cb=cb, mk=mk, cf=cf)
        in_maps.append(m)
    res = run_bass_kernel_spmd(nc, in_maps, core_ids=list(range(2 * B)))
    out = np.zeros((B, cfg.S, D), np.float32)
    for core in range(2 * B):
        b, r = core // 2, core % 2
        o = np.asarray(res.results[core]["out"], np.float32).reshape(cfg.NG, 512, D)
        out[b].reshape(cfg.NGG, 512, D)[r::2] = o
    return out


def kernel(**inputs):
    cfg = Cfg(S=8192, depth=4, npairs=4)
    return run(cfg, inputs)
```
